# Optimizing a Trainium2 kernel written in Bass

```python
import jax, jax.numpy as jnp
from jax import lax
import numpy as np

D_MODEL = 2048
BATCH = 2
SEQ = 16384
DEPTH = 1

POOL_WIDTH = D_MODEL // 2
POOL_GROUPS = 4
POOL_WINDOWS = (2, 4, 8, 16)
POOL_GROUP_DIM = POOL_WIDTH // POOL_GROUPS
HEAD_DIM = 128
N_HEADS = (D_MODEL // 2) // HEAD_DIM
ATTN_WIDTH = N_HEADS * HEAD_DIM
Q_BLOCK = 128
N_BRANCHES = 2
OFF_Q = POOL_WIDTH
OFF_K = OFF_Q + ATTN_WIDTH
OFF_V = OFF_K + ATTN_WIDTH
OFF_F = OFF_V + ATTN_WIDTH
OFF_GATE = OFF_F + N_HEADS
IN_WIDTH = OFF_GATE + N_BRANCHES * D_MODEL
N_EXPERTS = 64
TOP_K = 6
EXPERT_DIM = 1408 * D_MODEL // 2048
SHARED_DIM = 2 * EXPERT_DIM
ROUTE_SCALE = 1.0
EXPERT_BLOCK = 256
ALPHA = (2 * DEPTH) ** 0.25
BETA = (8 * DEPTH) ** -0.25
LN_EPS = 1e-5

kernel_name = "pool_fox_gated_hybrid_moe_deepnorm"


def layer_norm(x, g, b):
    x32 = x.astype(jnp.float32)
    mu = jnp.mean(x32, axis=-1, keepdims=True)
    var = jnp.mean(jnp.square(x32 - mu), axis=-1, keepdims=True)
    y = (x32 - mu) * lax.rsqrt(var + LN_EPS) * g.astype(jnp.float32) + b.astype(jnp.float32)
    return y.astype(x.dtype)


def pool_mixer(u, pool_w, pool_scale):
    B, S, _ = u.shape
    u32 = u.astype(jnp.float32)
    c = jnp.cumsum(u32, axis=1)
    c = jnp.concatenate([jnp.zeros_like(c[:, :1]), c], axis=1)
    t1 = jnp.arange(1, S + 1)
    outs = []
    for g, w in enumerate(POOL_WINDOWS):
        sl = slice(g * POOL_GROUP_DIM, (g + 1) * POOL_GROUP_DIM)
        cg = c[:, :, sl]
        lo = jnp.maximum(t1 - w, 0)
        cnt = jnp.minimum(t1, w).astype(jnp.float32)
        mean = (cg[:, 1:] - cg[:, lo]) / cnt[None, :, None]
        outs.append(mean - u32[:, :, sl])
    p = jnp.stack(outs, axis=2).astype(u.dtype)
    y = jnp.einsum('bsgc,gcd->bsgd', p, pool_w).reshape(B, S, POOL_WIDTH)
    return y * pool_scale


def forgetting_attention(q, k, v, log_f):
    B, S, H, Dh = q.shape
    F = jnp.cumsum(log_f, axis=1).transpose(0, 2, 1)
    qh = q.transpose(0, 2, 1, 3) * (Dh ** -0.5)
    kh = k.transpose(0, 2, 1, 3)
    vh = v.transpose(0, 2, 1, 3)
    k_pos = jnp.arange(S)
    n_blk = S // Q_BLOCK

    def block(i):
        start = i * Q_BLOCK
        qb = lax.dynamic_slice_in_dim(qh, start, Q_BLOCK, axis=2)
        fq = lax.dynamic_slice_in_dim(F, start, Q_BLOCK, axis=2)
        s = jnp.einsum('bhqd,bhkd->bhqk', qb, kh).astype(jnp.float32)
        s = s + fq[..., :, None] - F[..., None, :]
        q_pos = start + jnp.arange(Q_BLOCK)
        s = jnp.where(k_pos[None, :] <= q_pos[:, None], s, -jnp.inf)
        p = jax.nn.softmax(s, axis=-1).astype(vh.dtype)
        return jnp.einsum('bhqk,bhkd->bhqd', p, vh)

    o = lax.map(block, jnp.arange(n_blk))
    return o.transpose(1, 0, 3, 2, 4).reshape(B, S, H * Dh)


def hybrid_mixer(x, w_in, b_forget, pool_w, pool_scale, w_branch_pool, w_branch_attn, w_out):
    B, S, _ = x.shape
    z = jnp.einsum('bsd,de->bse', x, w_in)
    u_pool = z[..., :OFF_Q]
    q = z[..., OFF_Q:OFF_K].reshape(B, S, N_HEADS, HEAD_DIM)
    k = z[..., OFF_K:OFF_V].reshape(B, S, N_HEADS, HEAD_DIM)
    v = z[..., OFF_V:OFF_F].reshape(B, S, N_HEADS, HEAD_DIM)
    log_f = jax.nn.log_sigmoid(z[..., OFF_F:OFF_GATE].astype(jnp.float32) + b_forget.astype(jnp.float32))
    gates = jax.nn.sigmoid(z[..., OFF_GATE:].astype(jnp.float32)).astype(x.dtype)
    gates = gates.reshape(B, S, N_BRANCHES, D_MODEL)
    y_pool = jnp.einsum('bsc,cd->bsd', pool_mixer(u_pool, pool_w, pool_scale), w_branch_pool)
    y_attn = jnp.einsum('bsc,cd->bsd', forgetting_attention(q, k, v, log_f), w_branch_attn)
    merged = gates[:, :, 0] * y_pool + gates[:, :, 1] * y_attn
    return jnp.einsum('bsd,de->bse', merged, w_out)


def swiglu(h, w1, w3, w2):
    return (jax.nn.silu(h @ w1) * (h @ w3)) @ w2


def routed_experts(h, w_router, router_bias, w1, w3, w2):
    N, D = h.shape
    scores = jax.nn.sigmoid(jnp.dot(h, w_router).astype(jnp.float32))
    _, idx = lax.top_k(scores + router_bias.astype(jnp.float32), TOP_K)
    gate = jnp.take_along_axis(scores, idx, axis=1)
    gate = gate / jnp.sum(gate, axis=1, keepdims=True) * ROUTE_SCALE
    A = N * TOP_K
    e_flat = idx.reshape(A)
    tok_flat = jnp.repeat(jnp.arange(N, dtype=jnp.int32), TOP_K)
    g_flat = gate.reshape(A)
    order = jnp.argsort(e_flat)
    e_s, tok_s, g_s = e_flat[order], tok_flat[order], g_flat[order]
    counts = jnp.zeros((N_EXPERTS,), jnp.int32).at[e_flat].add(1)
    padded = (counts + EXPERT_BLOCK - 1) // EXPERT_BLOCK * EXPERT_BLOCK
    start = jnp.cumsum(counts) - counts
    pend = jnp.cumsum(padded)
    pstart = pend - padded
    dest = pstart[e_s] + (jnp.arange(A, dtype=jnp.int32) - start[e_s])
    n_blocks = -(-A // EXPERT_BLOCK) + N_EXPERTS
    R = n_blocks * EXPERT_BLOCK
    buf_tok = jnp.zeros((R,), jnp.int32).at[dest].set(tok_s).reshape(n_blocks, EXPERT_BLOCK)
    buf_gate = jnp.zeros((R,), jnp.float32).at[dest].set(g_s).reshape(n_blocks, EXPERT_BLOCK)
    blk_expert = jnp.searchsorted(pend, jnp.arange(n_blocks, dtype=jnp.int32) * EXPERT_BLOCK, side='right')
    blk_expert = jnp.minimum(blk_expert, N_EXPERTS - 1)

    def body(acc, xs):
        e, tok, g = xs
        y = swiglu(h[tok], w1[e], w3[e], w2[e])
        return acc.at[tok].add(y.astype(jnp.float32) * g[:, None]), None

    acc, _ = lax.scan(body, jnp.zeros((N, D), jnp.float32), (blk_expert, buf_tok, buf_gate))
    return acc.astype(h.dtype)


def setup_inputs(seed: int = 0) -> dict:
    key = jax.random.key(seed)
    ks = jax.random.split(key, 20)
    L, D, E, F, FS = DEPTH, D_MODEL, N_EXPERTS, EXPERT_DIM, SHARED_DIM
    nrm = jax.random.normal
    x = nrm(ks[0], (BATCH, SEQ, D), jnp.float32)
    w_in = nrm(ks[1], (L, D, IN_WIDTH), jnp.float32) * D ** -0.5
    w_in = w_in.at[:, :, OFF_V:OFF_F].multiply(BETA)
    b_forget = jax.random.uniform(ks[2], (L, N_HEADS), jnp.float32, 1.0, 5.0)
    pool_w = nrm(ks[3], (L, POOL_GROUPS, POOL_GROUP_DIM, POOL_GROUP_DIM), jnp.float32) * POOL_GROUP_DIM ** -0.5
    pool_scale = 1.0 + 0.1 * nrm(ks[4], (L, POOL_WIDTH), jnp.float32)
    w_branch_pool = nrm(ks[5], (L, POOL_WIDTH, D), jnp.float32) * POOL_WIDTH ** -0.5 * BETA
    w_branch_attn = nrm(ks[6], (L, ATTN_WIDTH, D), jnp.float32) * ATTN_WIDTH ** -0.5 * BETA
    w_out = nrm(ks[7], (L, D, D), jnp.float32) * D ** -0.5 * BETA
    ln1_g = 1.0 + 0.05 * nrm(ks[8], (L, D), jnp.float32)
    ln1_b = 0.02 * nrm(ks[9], (L, D), jnp.float32)
    w_router = nrm(ks[10], (L, D, E), jnp.float32) * D ** -0.5
    router_bias = 0.01 * nrm(ks[11], (L, E), jnp.float32)
    w1 = nrm(ks[12], (L, E, D, F), jnp.float32) * D ** -0.5 * BETA
    w3 = nrm(ks[13], (L, E, D, F), jnp.float32) * D ** -0.5 * BETA
    w2 = nrm(ks[14], (L, E, F, D), jnp.float32) * F ** -0.5 * BETA
    w_shared1 = nrm(ks[15], (L, D, FS), jnp.float32) * D ** -0.5 * BETA
    w_shared3 = nrm(ks[16], (L, D, FS), jnp.float32) * D ** -0.5 * BETA
    w_shared2 = nrm(ks[17], (L, FS, D), jnp.float32) * FS ** -0.5 * BETA
    ln2_g = 1.0 + 0.05 * nrm(ks[18], (L, D), jnp.float32)
    ln2_b = 0.02 * nrm(ks[19], (L, D), jnp.float32)
    return {"x": x, "w_in": w_in, "b_forget": b_forget, "pool_w": pool_w, "pool_scale": pool_scale,
            "w_branch_pool": w_branch_pool, "w_branch_attn": w_branch_attn, "w_out": w_out,
            "ln1_g": ln1_g, "ln1_b": ln1_b, "w_router": w_router, "router_bias": router_bias,
            "w1": w1, "w3": w3, "w2": w2, "w_shared1": w_shared1, "w_shared3": w_shared3,
            "w_shared2": w_shared2, "ln2_g": ln2_g, "ln2_b": ln2_b}


def reference(x, w_in, b_forget, pool_w, pool_scale, w_branch_pool, w_branch_attn, w_out,
              ln1_g, ln1_b, w_router, router_bias, w1, w3, w2, w_shared1, w_shared3,
              w_shared2, ln2_g, ln2_b):
    B, S, D = x.shape
    for l in range(DEPTH):
        mix = hybrid_mixer(x, w_in[l], b_forget[l], pool_w[l], pool_scale[l],
                           w_branch_pool[l], w_branch_attn[l], w_out[l])
        x = layer_norm(ALPHA * x + mix, ln1_g[l], ln1_b[l])
        h = x.reshape(B * S, D)
        ffn = swiglu(h, w_shared1[l], w_shared3[l], w_shared2[l]) + \
            routed_experts(h, w_router[l], router_bias[l], w1[l], w3[l], w2[l])
        x = layer_norm(ALPHA * x + ffn.reshape(B, S, D), ln2_g[l], ln2_b[l])
    return x
```

```python
from contextlib import ExitStack
import numpy as np
import concourse.bass as bass
import concourse.mybir as mybir
from concourse.bass_utils import run_bass_kernel_spmd

F32 = mybir.dt.float32
BF16 = mybir.dt.bfloat16
I32 = mybir.dt.int32
AF = mybir.ActivationFunctionType
ALU = mybir.AluOpType
AX = mybir.AxisListType
WINDOWS = (2, 4, 8, 16)
LN_EPS = 1e-5
ALPHA = 2.0 ** 0.25
NEG = -30000.0


class Cfg:
    def __init__(self, D=2048, TO=4096, E=64, TOPK=6, F=1408, FS=2816, C=512, TB=256, NSH=4):
        self.D, self.TO, self.E, self.TOPK, self.F, self.FS, self.C, self.TB = D, TO, E, TOPK, F, FS, C, TB
        self.NSH = NSH
        self.NV = self.NSH * TO
        self.KC = D // 128
        self.PW = D // 2
        self.PWC = self.PW // 128
        self.GD = self.PW // 4
        self.GDC = self.GD // 128
        self.H = (D // 2) // 128
        self.AW = self.H * 128
        self.OFF_Q = self.PW
        self.OFF_K = self.OFF_Q + self.AW
        self.OFF_V = self.OFF_K + self.AW
        self.OFF_F = self.OFF_V + self.AW
        self.OFF_G = self.OFF_F + self.H
        self.INW = self.OFF_G + 2 * D
        self.FC = F // 128
        self.FSC = FS // 128
        self.NKT = self.NV // 128
        self.NT = TO // 128
        self.NQS = TO // 256


class Buf:
    __slots__ = ("name", "t", "last_w", "readers", "dsem", "track")

    def __init__(self, name, t=None, track=True):
        self.name = name
        self.t = t
        self.last_w = None
        self.readers = {}
        self.dsem = None
        self.track = track

    def __getitem__(self, k):
        return self.t[k]


class Ctx:
    ENG = ("pe", "act", "dve", "pool", "sp")
    ARENA = 103000

    def __init__(self, nc):
        self.nc = nc
        self.es = ExitStack()
        self.q = {e: [] for e in self.ENG}
        self.cnt = {e: 0 for e in self.ENG}
        self.dcnt = {}
        self.ndsem = 0
        self.arena = self.es.enter_context(nc.sbuf_tensor("arena", [128, self.ARENA], BF16))
        self.ptr = 0
        self.mark = 0

    def alloc(self, name, shape, dt):
        n = int(np.prod(shape[1:]))
        nb = n * (2 if dt != BF16 else 1)
        self.ptr = (self.ptr + 15) // 16 * 16
        assert self.ptr + nb <= self.ARENA, (name, self.ptr, nb)
        v = self.arena[0:shape[0], self.ptr:self.ptr + nb]
        self.ptr += nb
        if dt != BF16:
            v = v.bitcast(dt)
        if len(shape) == 3:
            v = v.rearrange("p (a b) -> p a b", a=shape[1])
        elif len(shape) == 4:
            v = v.rearrange("p (a b c) -> p a b c", a=shape[1], b=shape[2])
        return Buf(name, v)

    def ps(self, name, shape, dt=F32):
        t = self.es.enter_context(self.nc.psum_tensor(name, list(shape), dt))
        return Buf(name, t)

    def phase(self):
        self.barrier()
        self.ptr = self.mark
        self.ndsem = self.ndsem_mark

    def keep(self):
        self.mark = self.ptr
        self.ndsem_mark = self.ndsem

    def barrier(self):
        allw = [(("c", e), v) for e, v in self.cnt.items() if v] + [(("d", n), v) for n, v in self.dcnt.items()]
        for e in self.ENG:
            self.q[e].append((list(allw), None, None, 0))

    def _deps(self, eng, reads, writes):
        w = {}

        def add(tok):
            if tok is None:
                return
            k, v = tok
            if k[0] == "c" and k[1] == eng and eng == "pe":
                return
            if k[0] == "d":
                v = self.dcnt[k[1]]
            if w.get(k, 0) < v:
                w[k] = v
        for r in reads:
            add(r.last_w)
        for b in writes:
            add(b.last_w)
            for k, v in b.readers.items():
                add((k, v))
        return list(w.items())

    def _commit(self, tok, reads, writes):
        k, v = tok
        for r in reads:
            if r.readers.get(k, 0) < v:
                r.readers[k] = v
        for b in writes:
            b.last_w = tok
            b.readers = {}

    def op(self, eng, fn, reads=(), writes=()):
        waits = self._deps(eng, reads, writes)
        self.cnt[eng] += 1
        tok = (("c", eng), self.cnt[eng])
        self.q[eng].append((waits, fn, tok[0], 1))
        self._commit(tok, reads, writes)

    def dma(self, eng, fn, reads=(), writes=(), sem=None):
        reads = [b for b in reads if b.track]
        writes = [b for b in writes if b.track]
        if sem is None:
            sem = (list(writes) + list(reads))[0]
        if sem.dsem is None:
            sem.dsem = self.ndsem
            self.ndsem += 1
        name = sem.dsem
        waits = self._deps(eng, reads, writes)
        self.dcnt[name] = self.dcnt.get(name, 0) + 16
        tok = (("d", name), self.dcnt[name])
        self.q[eng].append((waits, fn, tok[0], 16))
        self._commit(tok, reads, writes)

    def emit(self):
        nc = self.nc
        self.barrier()
        sems = {}
        for e in self.ENG:
            if self.cnt[e]:
                sems[("c", e)] = self.es.enter_context(nc.semaphore("c_" + e))
        for name in self.dcnt:
            sems[("d", name)] = self.es.enter_context(nc.semaphore("d_%d" % name))
        block = self.es.enter_context(nc.Block())

        def run(e, engobj):
            waited = {}
            for waits, fn, sk, amt in self.q[e]:
                for k, v in waits:
                    if waited.get(k, 0) < v:
                        engobj.wait_ge(sems[k], v)
                        waited[k] = v
                if fn is not None:
                    fn(engobj).then_inc(sems[sk], amt)

        block.tensor(lambda x: run("pe", x))
        block.scalar(lambda x: run("act", x))
        block.vector(lambda x: run("dve", x))
        block.gpsimd(lambda x: run("pool", x))
        block.sync(lambda x: run("sp", x))
        self.es.close()


def build(cfg):
    nc = bass.Bass("TRN2", target_bir_lowering=False)
    D, TO, E, TOPK, F, FS, C, TB = cfg.D, cfg.TO, cfg.E, cfg.TOPK, cfg.F, cfg.FS, cfg.C, cfg.TB
    NV, KC, PW, PWC, GD, GDC, H, AW = cfg.NV, cfg.KC, cfg.PW, cfg.PWC, cfg.GD, cfg.GDC, cfg.H, cfg.AW
    FC, FSC, NKT, NT, NQS = cfg.FC, cfg.FSC, cfg.NKT, cfg.NT, cfg.NQS
    NCST = 5 * 128 + E

    def ein(name, shape, dt=F32):
        return Buf(name, nc.dram_tensor(name, list(shape), dt, kind="ExternalInput").ap(), track=False)

    def scr(name, shape, dt):
        return Buf(name, nc.dram_tensor(name, list(shape), dt, kind="Internal").ap(), track=False)

    xv = ein("xv", [NV, D]); kbias_d = ein("kbias", [128, NKT]); corr_d = ein("corr", [128, 64])
    cst_d = ein("cst", [128, NCST]); tokid_d = ein("tokid", [128, NT], I32)
    w_in = ein("w_in", [D, cfg.INW]); bfor_d = ein("bfor", [128, H]); pool_w = ein("pool_w", [4, GD, GD])
    pscale_d = ein("pscale", [128, PWC]); wbp = ein("wbp", [PW, D]); wba = ein("wba", [AW, D])
    w_out = ein("w_out", [D, D]); ln1g_d = ein("ln1g", [128, D]); ln1b_d = ein("ln1b", [128, D])
    w_router = ein("w_router", [D, E]); rbias_d = ein("rbias", [128, E])
    w1 = ein("w1", [E, D, F]); w3 = ein("w3", [E, D, F]); w2 = ein("w2", [E, F, D])
    ws1 = ein("ws1", [D, FS]); ws3 = ein("ws3", [D, FS]); ws2 = ein("ws2", [FS, D])
    ln2g_d = ein("ln2g", [128, D]); ln2b_d = ein("ln2b", [128, D])
    out_d = Buf("out", nc.dram_tensor("out", [NV, D], F32, kind="ExternalOutput").ap(), track=False)

    Ks = scr("Ks", [H, 128, NV], BF16); Vs = scr("Vs", [H, NV, 128], BF16)
    Qs = scr("Qs", [H, 128, NV], BF16); As = scr("As", [H, 128, NV], BF16)
    Gs = scr("Gs", [128, NKT * H], F32)
    H32 = scr("H32", [TO, D], F32); H16 = scr("H16", [TO, D], BF16); Base = scr("Base", [TO, D], F32)
    Tab = scr("Tab", [E * C, 1], I32); Tab.track = True; Yb = scr("Yb", [E * C, D], F32)

    cx = Ctx(nc)
    sp, pool = "sp", "pool"

    def dma(eng, out_b, out_ap, in_b, in_ap, sem=None, **kw):
        cx.dma(eng, lambda e: e.dma_start(out=out_ap, in_=in_ap, **kw), reads=[in_b], writes=[out_b], sem=sem)

    def mm(ob, oap, lb, lap, rb, rap, start, stop):
        cx.op("pe", lambda e: e.matmul(oap, lhsT=lap, rhs=rap, start=start, stop=stop), reads=[lb, rb], writes=[ob])

    rr = {"n": 0}

    def evac(ob, oap, ib, iap):
        rr["n"] += 1
        if rr["n"] % 2:
            cx.op("act", lambda e: e.copy(out=oap, in_=iap), reads=[ib], writes=[ob])
        else:
            cx.op("dve", lambda e: e.tensor_copy(out=oap, in_=iap), reads=[ib], writes=[ob])

    PB = [cx.ps("pb%d" % i, [128, 512], F32) for i in range(6)]
    TBK = [cx.ps("tb%d" % i, [128, 1024], BF16) for i in range(2)]
    tcnt = {"n": 0}

    cstf = cx.alloc("cstf", [128, NCST], F32)
    ident32 = cstf[:, 0:128]; U32 = cstf[:, 128:256]; Us32 = cstf[:, 256:384]; ones32 = cstf[:, 384:512]
    E032 = cstf[:, 512:640]; iotaE = cstf[:, 640:640 + E]
    cstb = cx.alloc("cstb", [128, 384], BF16)
    identb = cstb[:, 0:128]; trib = cstb[:, 128:256]; onesb = cstb[:, 256:384]
    tokid = cx.alloc("tokid", [128, NT], I32)
    gates = cx.alloc("gates", [128, NT, 8], F32)
    slots = cx.alloc("slots", [128, NT, 8], I32)
    dma(sp, cstf, cstf[:], cst_d, cst_d[:, :])
    dma(sp, tokid, tokid[:], tokid_d, tokid_d[:, :])
    cx.op("dve", lambda e: e.tensor_copy(out=cstb[:, 0:256], in_=cstf[:, 0:256]), reads=[cstf], writes=[cstb])
    cx.op("dve", lambda e: e.tensor_copy(out=cstb[:, 256:384], in_=cstf[:, 384:512]), reads=[cstf], writes=[cstb])
    cx.keep()

    def transpose_rows(src_b, src_ap, nrows, nchunks, dst_b, dst_fn, ident=None):
        for g0 in range(0, nchunks, 8):
            n = min(8, nchunks - g0)
            tb = TBK[tcnt["n"] % 2]; tcnt["n"] += 1
            for i in range(n):
                c = g0 + i
                cx.op("pe", lambda e, tb=tb, i=i, c=c: e.transpose(
                    out=tb[:, i * 128:i * 128 + nrows], in_=src_ap[:, c * 128:(c + 1) * 128],
                    identity=identb[0:nrows, 0:nrows]), reads=[src_b, cstb], writes=[tb])
            evac(dst_b, dst_fn(g0, n), tb, tb[:, 0:n * 128].rearrange("p (c n) -> p c n", n=128)[:, :, 0:nrows])

    def wload(slot, shape_ap, src_b, src_ap):
        dma(pool, slot, shape_ap, src_b, src_ap)

    def kview(wb, r0, nchunks, c0, ncols):
        return wb.t[r0:r0 + nchunks * 128, c0:c0 + ncols].rearrange("(c p) n -> p c n", p=128)

    def layernorm(xb, xap, gb, bb, tag):
        FM = min(D, int(nc.vector.BN_STATS_FMAX))
        nst = D // FM
        st = cx.alloc("lnst" + tag, [128, nst, int(nc.vector.BN_STATS_DIM)], F32)
        mv = cx.alloc("lnmv" + tag, [128, int(nc.vector.BN_AGGR_DIM)], F32)
        sd = cx.alloc("lnsd" + tag, [128, 2], F32)

        def run(xb, xap):
            for c in range(nst):
                cx.op("dve", lambda e, c=c: e.bn_stats(out=st[:, c, :], in_=xap[:, c * FM:(c + 1) * FM]), reads=[xb], writes=[st])
            cx.op("dve", lambda e: e.bn_aggr(out=mv[:], in_=st[:]), reads=[st], writes=[mv])
            cx.op("act", lambda e: e.activation(out=sd[:, 0:1], in_=mv[:, 1:2], func=AF.Sqrt, bias=LN_EPS, scale=1.0), reads=[mv], writes=[sd])
            cx.op("dve", lambda e: e.reciprocal(out=sd[:, 1:2], in_=sd[:, 0:1]), reads=[sd], writes=[sd])
            cx.op("dve", lambda e: e.tensor_scalar(out=xap, in0=xap, scalar1=mv[:, 0:1], scalar2=sd[:, 1:2],
                                                  op0=ALU.subtract, op1=ALU.mult), reads=[xb, mv, sd], writes=[xb])
            cx.op("dve", lambda e: e.tensor_tensor(out=xap, in0=xap, in1=gb[:], op=ALU.mult), reads=[xb, gb], writes=[xb])
            cx.op("dve", lambda e: e.tensor_tensor(out=xap, in0=xap, in1=bb[:], op=ALU.add), reads=[xb, bb], writes=[xb])
        return run

    def xpass(which):
        cx.phase()
        XB = 512
        if which == "kvf":
            ncol = 2 * AW + H
            wres = cx.alloc("wres", [128, KC, ncol], BF16)
            for c in range(KC):
                wload(wres, wres[:, c, 0:2 * AW], w_in, w_in.t[c * 128:(c + 1) * 128, cfg.OFF_K:cfg.OFF_K + 2 * AW])
                wload(wres, wres[:, c, 2 * AW:ncol], w_in, w_in.t[c * 128:(c + 1) * 128, cfg.OFF_F:cfg.OFF_F + H])
            blocks = range(NV // XB)
            row_base = 0
        else:
            wres = cx.alloc("wres", [128, KC, AW], BF16)
            for c in range(KC):
                wload(wres, wres[:, c, :], w_in, w_in.t[c * 128:(c + 1) * 128, cfg.OFF_Q:cfg.OFF_Q + AW])
            blocks = range(NV // XB)
            row_base = 0
        xin = [cx.alloc("xin%d" % i, [128, D], F32) for i in range(3)]
        xbf = [cx.alloc("xbf%d" % i, [128, D], BF16) for i in range(2)]
        xT = [cx.alloc("xT%d" % i, [128, KC, XB], BF16) for i in range(2)]
        kst = [cx.alloc("kst%d" % i, [128, H, XB], BF16) for i in range(2)]
        if which == "kvf":
            vst = [cx.alloc("vst%d" % i, [128, XB // 128, AW], BF16) for i in range(2)]
            bfor = cx.alloc("bfor", [128, H], F32)
            dma(sp, bfor, bfor[:], bfor_d, bfor_d[:, :])
            Gtok = cx.alloc("Gtok", [128, NKT, H], F32)
            carry = cx.alloc("carry", [128, H], F32)
            cx.op("dve", lambda e: e.memset(carry[:], 0.0), writes=[carry])
            fb = [cx.alloc("fb%d" % i, [128, H], F32) for i in range(2)]
        ti = 0
        for bi, blk in enumerate(blocks):
            r0 = row_base + blk * XB
            xt = xT[bi % 2]
            for t in range(XB // 128):
                xi = xin[ti % 3]; xb = xbf[ti % 2]; ti += 1
                dma(sp, xi, xi[:], xv, xv.t[r0 + t * 128:r0 + (t + 1) * 128, :])
                if ti % 2:
                    cx.op("act", lambda e, xi=xi, xb=xb: e.copy(out=xb[:], in_=xi[:]), reads=[xi], writes=[xb])
                else:
                    cx.op("dve", lambda e, xi=xi, xb=xb: e.tensor_copy(out=xb[:], in_=xi[:]), reads=[xi], writes=[xb])
                transpose_rows(xb, xb[:], 128, KC, xt, lambda c0, n, xt=xt, t=t: xt[:, c0:c0 + n, t * 128:(t + 1) * 128])
            ks = kst[bi % 2]
            for h in range(H):
                pb = PB[h % 2]
                for c in range(KC):
                    mm(pb, pb[:, :], wres, wres[:, c, h * 128:(h + 1) * 128], xt, xt[:, c, :], c == 0, c == KC - 1)
                if which == "q":
                    rr["n"] += 1
                    cx.op("act", lambda e, ks=ks, h=h, pb=pb: e.mul(out=ks[:, h, :], in_=pb[:, :], mul=128.0 ** -0.5), reads=[pb], writes=[ks])
                else:
                    evac(ks, ks[:, h, :], pb, pb[:, :])
            if which == "q":
                dma(sp, Qs, Qs.t[:, :, blk * XB:(blk + 1) * XB].rearrange("h d t -> d h t"), ks, ks[:], sem=ks)
                continue
            dma(sp, Ks, Ks.t[:, :, r0:r0 + XB].rearrange("h d t -> d h t"), ks, ks[:], sem=ks)
            vs = vst[bi % 2]
            for t in range(XB // 128):
                for nb in range(0, AW, 512):
                    pb = PB[2 + (t + nb // 512) % 2]
                    for c in range(KC):
                        mm(pb, pb[:, :], xt, xt[:, c, t * 128:(t + 1) * 128], wres, wres[:, c, AW + nb:AW + nb + 512], c == 0, c == KC - 1)
                    evac(vs, vs[:, t, nb:nb + 512], pb, pb[:, :])
            for t in range(XB // 128):
                dma(sp, Vs, Vs.t[:, r0 + t * 128:r0 + (t + 1) * 128, :].rearrange("h p d -> p h d"), vs,
                    vs[:, t, :].rearrange("p (h d) -> p h d", d=128), sem=vs)
            for t in range(XB // 128):
                kt = (r0 // 128) + t
                pf = PB[4]; pg = PB[5]; f = fb[t % 2]
                for c in range(KC):
                    mm(pf, pf[:, 0:H], xt, xt[:, c, t * 128:(t + 1) * 128], wres, wres[:, c, 2 * AW:2 * AW + H], c == 0, c == KC - 1)
                cx.op("dve", lambda e, f=f, pf=pf: e.tensor_tensor(out=f[:], in0=pf[:, 0:H], in1=bfor[:], op=ALU.add), reads=[pf, bfor], writes=[f])
                cx.op("act", lambda e, f=f: e.activation(out=f[:], in_=f[:], func=AF.Exp, scale=-1.0), reads=[f], writes=[f])
                cx.op("act", lambda e, f=f: e.activation(out=f[:], in_=f[:], func=AF.Ln, bias=1.0, scale=1.0), reads=[f], writes=[f])
                mm(pg, pg[:, 0:H], cstf, U32, f, f[:], True, True)
                mm(pg, pg[:, 64:64 + H], cstf, ones32, f, f[:], True, True)
                cx.op("dve", lambda e, kt=kt, pg=pg: e.tensor_tensor(out=Gtok[:, kt, :], in0=pg[:, 0:H], in1=carry[:], op=ALU.add), reads=[pg, carry], writes=[Gtok])
                cx.op("dve", lambda e, pg=pg: e.tensor_tensor(out=carry[:], in0=pg[:, 64:64 + H], in1=carry[:], op=ALU.add), reads=[pg, carry], writes=[carry])
        if which == "kvf":
            dma(sp, Gs, Gs[:, :], Gtok, Gtok[:].rearrange("p a b -> p (a b)"), sem=Gtok)

    Wi16 = scr("Wi16", [D, PW + 2 * D], BF16); wbp16 = scr("wbp16", [PW, D], BF16); wba16 = scr("wba16", [AW, D], BF16)
    wo16 = scr("wo16", [D, D], BF16); ws1b = scr("ws1b", [D, FS], BF16); ws3b = scr("ws3b", [D, FS], BF16); ws2b = scr("ws2b", [FS, D], BF16)
    cx.phase()
    bnc = [cx.alloc("bnc%d" % i, [128, PW + 2 * D], BF16) for i in range(3)]
    bstate = {"n": 0}

    def precast(dst, src, nrows, segs):
        for r in range(0, nrows, 128):
            b = bnc[bstate["n"] % 3]; bstate["n"] += 1
            tot = 0
            for (sc, n, dc) in segs:
                dma(pool, b, b[:, dc:dc + n], src, src.t[r:r + 128, sc:sc + n])
                tot = max(tot, dc + n)
            dma(sp, dst, dst.t[r:r + 128, 0:tot], b, b[:, 0:tot], sem=b)

    precast(Wi16, w_in, D, [(0, PW, 0), (cfg.OFF_G, 2 * D, PW)])
    precast(wbp16, wbp, PW, [(0, D, 0)]); precast(wba16, wba, AW, [(0, D, 0)]); precast(wo16, w_out, D, [(0, D, 0)])
    precast(ws1b, ws1, D, [(0, FS, 0)]); precast(ws3b, ws3, D, [(0, FS, 0)]); precast(ws2b, ws2, FS, [(0, D, 0)])

    xpass("kvf")
    xpass("q")

    def shard_phases(sh):
        cx.phase()
        Gtok = cx.alloc("Gtok3", [128, NKT, H], F32)
        kb = cx.alloc("kb3", [128, NKT], F32)
        dma(sp, Gtok, Gtok[:].rearrange("p a b -> p (a b)"), Gs, Gs[:, :])
        dma(sp, kb, kb[:], kbias_d, kbias_d[:, :])
        KT = [cx.alloc("KT%d" % i, [128, NV], BF16) for i in range(2)]
        VH = [cx.alloc("VH%d" % i, [128, NKT, 128], BF16) for i in range(2)]
        QT = [cx.alloc("QT%d" % i, [128, TO], BF16) for i in range(2)]
        BT = [cx.alloc("BT%d" % i, [128, NKT, NQS], F32) for i in range(2)]
        gk = cx.alloc("gk", [128, NKT], F32)
        gm = cx.alloc("gm", [128, NQS], F32)
        Pt = [cx.alloc("P%d" % i, [128, 512], BF16) for i in range(4)]
        rec = cx.alloc("rec", [128, 512], F32)
        ost = [cx.alloc("ost%d" % i, [128, 512], BF16) for i in range(2)]
        KT0 = sh * TO // 128
        pcnt = 0
        qbcnt = 0

        def load_head(h):
            dma(sp, KT[h % 2], KT[h % 2][:], Ks, Ks.t[h, :, :])
            for t0 in range(0, NKT, 16):
                n = min(16, NKT - t0)
                dma(sp, VH[h % 2], VH[h % 2][:, t0:t0 + n, :], Vs,
                    Vs.t[h, t0 * 128:(t0 + n) * 128, :].rearrange("(t p) d -> p t d", p=128))
            dma(sp, QT[h % 2], QT[h % 2][:], Qs, Qs.t[h, :, sh * TO:(sh + 1) * TO])

        load_head(0)
        for h in range(H):
            if h + 1 < H:
                load_head(h + 1)
            kt_, vh_, qt_, bt_ = KT[h % 2], VH[h % 2], QT[h % 2], BT[h % 2]
            pgm = PB[4]
            mids = Gtok[:, KT0:KT0 + 2 * NQS, h].rearrange("p (q two) -> p q two", two=2)[:, :, 1]
            cx.op("dve", lambda e, h=h: e.tensor_tensor(out=gk[:], in0=Gtok[:, :, h], in1=kb[:], op=ALU.add), reads=[Gtok, kb], writes=[gk])
            cx.op("dve", lambda e, mids=mids: e.tensor_copy(out=gm[:], in_=mids), reads=[Gtok], writes=[gm])
            mm(pgm, pgm[:, 0:NQS], cstf, E032, gm, gm[:], True, True)
            cx.op("act", lambda e: e.copy(out=gm[:], in_=pgm[:, 0:NQS]), reads=[pgm], writes=[gm])
            cx.op("dve", lambda e, bt_=bt_: e.tensor_tensor(
                out=bt_[:], in0=gk[:].unsqueeze(2).to_broadcast([128, NKT, NQS]),
                in1=gm[:].unsqueeze(1).to_broadcast([128, NKT, NQS]), op=ALU.subtract), reads=[gk, gm], writes=[bt_])
            for qb in range(TO // 512):
                qbcnt += 1
                po = PB[qbcnt % 2]; pl = PB[4 + qbcnt % 2]
                nk = KT0 + 4 * qb + 4
                items = []
                for kt in range(nk):
                    col0 = max(0, kt - (KT0 + 4 * qb)) * 128
                    items.append((kt, col0))

                def emit_s(i):
                    kt, col0 = items[i]
                    pss = PB[2 + i % 2]
                    mm(pss, pss[:, col0:512], kt_, kt_[:, kt * 128:(kt + 1) * 128], qt_, qt_[:, qb * 512 + col0:(qb + 1) * 512], True, True)

                def emit_pv(i):
                    nonlocal pcnt
                    kt, col0 = items[i]
                    pss = PB[2 + i % 2]
                    p = Pt[pcnt % 4]; pcnt += 1
                    for half in range(2):
                        a = max(col0, half * 256); b = (half + 1) * 256
                        if a >= b:
                            continue
                        bap = bt_[:, kt, 2 * qb + half:2 * qb + half + 1]
                        cx.op("act", lambda e, a=a, b=b, p=p, pss=pss, bap=bap: e.activation(
                            out=p[:, a:b], in_=pss[:, a:b], func=AF.Exp, bias=bap, scale=1.0),
                            reads=[pss, bt_], writes=[p])
                    if kt >= KT0 + 4 * qb:
                        cx.op("pool", lambda e, p=p, col0=col0: e.tensor_tensor(out=p[:, col0:col0 + 128], in0=p[:, col0:col0 + 128], in1=trib, op=ALU.mult),
                              reads=[p, cstb], writes=[p])
                    mm(po, po[:, col0:512], vh_, vh_[:, kt, :], p, p[:, col0:512], i == 0, i == len(items) - 1)
                    mm(pl, pl[:, col0:512], cstb, onesb, p, p[:, col0:512], i == 0, i == len(items) - 1)

                for i in range(len(items) + 1):
                    if i < len(items):
                        emit_s(i)
                    if i >= 1:
                        emit_pv(i - 1)
                o = ost[qb % 2]
                cx.op("dve", lambda e, pl=pl: e.reciprocal(out=rec[:], in_=pl[:, :]), reads=[pl], writes=[rec])
                cx.op("dve", lambda e, o=o, po=po: e.tensor_tensor(out=o[:], in0=po[:, :], in1=rec[:], op=ALU.mult), reads=[po, rec], writes=[o])
                dma(sp, As, As.t[h, :, sh * TO + qb * 512:sh * TO + (qb + 1) * 512], o, o[:], sem=o)

        cx.phase()
        NB = TO // TB
        TT = TB // 128
        XW = TB + 16
        RS = max(KC, FSC, PWC, H) * 256
        ring = [cx.alloc("ring%d" % i, [128, RS], BF16) for i in range(4)]
        rcnt = {"n": 0}

        def rslot(nch, ncols):
            b = ring[rcnt["n"] % len(ring)]; rcnt["n"] += 1
            return b, b[:, 0:nch * ncols].rearrange("p (c n) -> p c n", n=ncols)

        xin4 = cx.alloc("xin4", [128, TT, D], F32)
        xinh = cx.alloc("xinh", [128, D], F32)
        xbf4 = cx.alloc("xbf4", [128, D], BF16)
        xTe = cx.alloc("xTe", [128, KC, XW], BF16)
        u = cx.alloc("u", [128, PWC, XW], F32)
        sA = cx.alloc("sA", [128, PWC, XW], F32)
        sB = cx.alloc("sB", [128, PWC, XW], F32)
        pT = cx.alloc("pT", [128, PWC, TB], BF16)
        ypT = cx.alloc("ypT", [128, PWC, TB], BF16)
        atT = cx.alloc("atT", [128, H, TB], BF16)
        mg = cx.alloc("mg", [128, KC, TB], BF16)
        sg = [cx.alloc("sg%d" % i, [128, TB], F32) for i in range(2)]
        tmpf = cx.alloc("tmpf", [128, TB], F32)
        hb = cx.alloc("hb", [128, D], BF16)
        hT = cx.alloc("hT", [128, KC, TB], BF16)
        hT32 = cx.alloc("hT32", [128, KC, 128], F32)
        aT = cx.alloc("aT", [128, FSC, TB], BF16)
        ln1g = cx.alloc("ln1g", [128, D], F32); ln1b = cx.alloc("ln1b", [128, D], F32)
        wr = cx.alloc("wr", [128, KC, E], F32)
        pw = cx.alloc("pw", [128, 4 * GDC, GD], BF16)
        psc = cx.alloc("psc", [128, PWC], F32)
        corr = cx.alloc("corr", [128, 4, 16], F32)
        rbias = cx.alloc("rbias", [128, E], F32)
        rcar = cx.alloc("rcar", [128, E], F32)
        rt = {k: cx.alloc("rt_" + k, [128, E], F32) for k in ("sc", "selb", "mask", "pos", "oh", "tmp")}
        m8 = cx.alloc("m8", [128, 8], F32)
        r3 = cx.alloc("r3", [128, 3, 8], F32)
        rsum = cx.alloc("rsum", [128, 2], F32)
        slf = cx.alloc("slf", [128, 8], F32)
        zt = cx.alloc("zt", [128, E * C // 128], I32)
        ln1 = layernorm(None, None, ln1g, ln1b, "1")
        dma(sp, ln1g, ln1g[:], ln1g_d, ln1g_d[:, :]); dma(sp, ln1b, ln1b[:], ln1b_d, ln1b_d[:, :])
        dma(sp, wr, wr[:], w_router, w_router.t[:, :].rearrange("(c p) e -> p c e", p=128))
        dma(sp, psc, psc[:], pscale_d, pscale_d[:, :])
        dma(sp, corr, corr[:].rearrange("p a b -> p (a b)"), corr_d, corr_d[:, :])
        dma(sp, rbias, rbias[:], rbias_d, rbias_d[:, :])
        for g in range(4):
            wload(pw, pw[:, g * GDC:(g + 1) * GDC, :], pool_w, pool_w.t[g, :, :].rearrange("(c p) n -> p c n", p=128))
        cx.op("dve", lambda e: e.memset(rcar[:], 0.0), writes=[rcar])
        cx.op("dve", lambda e: e.memset(zt[:], 0), writes=[zt])
        dma(sp, Tab, Tab.t[:, :].rearrange("(p i) o -> p (i o)", p=128), zt, zt[:], sem=zt)

        units = []

        def add_unit(load, comp):
            units.append((load, comp))

        def block_units(bk):
            r0 = sh * TO + bk * TB
            o0 = bk * TB
            st = {}

            def prologue():
                if r0 == 0:
                    cx.op("dve", lambda e: e.memset(xTe[:, :, 0:16], 0.0), writes=[xTe])
                else:
                    dma(sp, xinh, xinh[0:16, :], xv, xv.t[r0 - 16:r0, :])
                    cx.op("dve", lambda e: e.tensor_copy(out=xbf4[0:16, :], in_=xinh[0:16, :]), reads=[xinh], writes=[xbf4])
                    transpose_rows(xbf4, xbf4[0:16, :], 16, KC, xTe, lambda c0, n: xTe[:, c0:c0 + n, 0:16])
                for t in range(TT):
                    dma(sp, xin4, xin4[:, t, :], xv, xv.t[r0 + t * 128:r0 + (t + 1) * 128, :])
                    cx.op("act", lambda e, t=t: e.copy(out=xbf4[:], in_=xin4[:, t, :]), reads=[xin4], writes=[xbf4])
                    transpose_rows(xbf4, xbf4[:], 128, KC, xTe, lambda c0, n, t=t: xTe[:, c0:c0 + n, 16 + t * 128:16 + (t + 1) * 128])
                dma(sp, atT, atT[:], As, As.t[:, :, sh * TO + o0:sh * TO + o0 + TB].rearrange("h d t -> d h t"))

            for j in range(PW // 256):
                def load(j=j):
                    st["wp", j] = rslot(KC, 256)
                    b, v = st["wp", j]
                    wload(b, v, Wi16, kview(Wi16, 0, KC, j * 256, 256))

                def comp(j=j):
                    if j == 0:
                        prologue()
                    b, v = st["wp", j]
                    for oc in range(2):
                        pb = PB[oc]
                        for c in range(KC):
                            mm(pb, pb[:, 0:XW], b, v[:, c, oc * 128:(oc + 1) * 128], xTe, xTe[:, c, :], c == 0, c == KC - 1)
                        evac(u, u[:, 2 * j + oc, :], pb, pb[:, 0:XW])
                add_unit(load, comp)

            def pool_mix():
                cpg = GDC
                cx.op("dve", lambda e: e.tensor_tensor(out=sA[:, :, 1:XW], in0=u[:, :, 1:XW], in1=u[:, :, 0:XW - 1], op=ALU.add), reads=[u], writes=[sA])
                cx.op("dve", lambda e: e.tensor_tensor(out=sB[:, cpg:, 3:XW], in0=sA[:, cpg:, 3:XW], in1=sA[:, cpg:, 1:XW - 2], op=ALU.add), reads=[sA], writes=[sB])
                cx.op("dve", lambda e: e.tensor_tensor(out=sA[:, 2 * cpg:, 7:XW], in0=sB[:, 2 * cpg:, 7:XW], in1=sB[:, 2 * cpg:, 3:XW - 4], op=ALU.add), reads=[sB], writes=[sA])
                cx.op("dve", lambda e: e.tensor_tensor(out=sB[:, 3 * cpg:, 15:XW], in0=sA[:, 3 * cpg:, 15:XW], in1=sA[:, 3 * cpg:, 7:XW - 8], op=ALU.add), reads=[sA], writes=[sB])
                srcs = [sA, sB, sA, sB]
                for g in range(4):
                    s = srcs[g]; cs = slice(g * cpg, (g + 1) * cpg)
                    cx.op("dve", lambda e, s=s, cs=cs, g=g: e.scalar_tensor_tensor(
                        out=pT[:, cs, :], in0=s[:, cs, 16:XW], scalar=1.0 / WINDOWS[g], in1=u[:, cs, 16:XW], op0=ALU.mult, op1=ALU.subtract),
                        reads=[s, u], writes=[pT])
                    if bk == 0 and sh == 0:
                        cx.op("dve", lambda e, s=s, cs=cs, g=g: e.tensor_tensor(
                            out=s[:, cs, 16:32], in0=s[:, cs, 16:32], in1=corr[:, g:g + 1, :].to_broadcast([128, cpg, 16]), op=ALU.mult),
                            reads=[s, corr], writes=[s])
                        cx.op("dve", lambda e, s=s, cs=cs: e.tensor_tensor(out=pT[:, cs, 0:16], in0=s[:, cs, 16:32], in1=u[:, cs, 16:32], op=ALU.subtract),
                              reads=[s, u], writes=[pT])
                for g in range(4):
                    for oc in range(GDC):
                        pb = PB[2 + (g * GDC + oc) % 2]
                        for ic in range(GDC):
                            mm(pb, pb[:, 0:TB], pw, pw[:, g * GDC + ic, oc * 128:(oc + 1) * 128], pT, pT[:, g * GDC + ic, :], ic == 0, ic == GDC - 1)
                        ch = g * GDC + oc
                        cx.op("act", lambda e, ch=ch, pb=pb: e.activation(out=ypT[:, ch, :], in_=pb[:, 0:TB], func=AF.Copy, scale=psc[:, ch:ch + 1]),
                              reads=[pb, psc], writes=[ypT])

            for br in range(2):
                for j in range(D // 256):
                    def load(j=j, br=br):
                        wb_, nk = (wbp16, PWC) if br == 0 else (wba16, H)
                        st["wb", br, j] = rslot(nk, 256)
                        b, v = st["wb", br, j]
                        wload(b, v, wb_, kview(wb_, 0, nk, j * 256, 256))
                        st["wg", br, j] = rslot(KC, 256)
                        b, v = st["wg", br, j]
                        wload(b, v, Wi16, kview(Wi16, 0, KC, PW + br * D + j * 256, 256))

                    def comp(j=j, br=br):
                        if br == 0 and j == 0:
                            pool_mix()
                        src, nk = (ypT, PWC) if br == 0 else (atT, H)
                        b, v = st["wb", br, j]; bg, vg = st["wg", br, j]
                        for oc in range(2):
                            ch = 2 * j + oc
                            pa = PB[oc]; pg = PB[2 + oc]; s = sg[oc]
                            for c in range(nk):
                                mm(pa, pa[:, 0:TB], b, v[:, c, oc * 128:(oc + 1) * 128], src, src[:, c, :], c == 0, c == nk - 1)
                            for c in range(KC):
                                mm(pg, pg[:, 0:TB], bg, vg[:, c, oc * 128:(oc + 1) * 128], xTe, xTe[:, c, 16:XW], c == 0, c == KC - 1)
                            cx.op("act", lambda e, s=s, pg=pg: e.activation(out=s[:], in_=pg[:, 0:TB], func=AF.Sigmoid), reads=[pg], writes=[s])
                            if br == 0:
                                cx.op("dve", lambda e, s=s, pa=pa, ch=ch: e.tensor_tensor(out=mg[:, ch, :], in0=pa[:, 0:TB], in1=s[:], op=ALU.mult), reads=[pa, s], writes=[mg])
                            else:
                                cx.op("dve", lambda e, s=s, pa=pa: e.tensor_tensor(out=tmpf[:], in0=pa[:, 0:TB], in1=s[:], op=ALU.mult), reads=[pa, s], writes=[tmpf])
                                cx.op("dve", lambda e, ch=ch: e.tensor_tensor(out=mg[:, ch, :], in0=mg[:, ch, :], in1=tmpf[:], op=ALU.add), reads=[mg, tmpf], writes=[mg])
                    add_unit(load, comp)

            def ln_router():
                for t in range(TT):
                    tok0 = o0 + t * 128
                    tile_i = tok0 // 128
                    ln1(xin4, xin4[:, t, :])
                    cx.op("act", lambda e, t=t: e.copy(out=hb[:], in_=xin4[:, t, :]), reads=[xin4], writes=[hb])
                    dma(sp, H16, H16.t[tok0:tok0 + 128, :], hb, hb[:], sem=hb)
                    transpose_rows(hb, hb[:], 128, KC, hT, lambda c0, n, t=t: hT[:, c0:c0 + n, t * 128:(t + 1) * 128])
                    for g0 in range(0, KC, 4):
                        pb = PB[4 + (g0 // 4) % 2]
                        for i in range(4):
                            c = g0 + i
                            cx.op("pe", lambda e, pb=pb, i=i, c=c, t=t: e.transpose(out=pb[:, i * 128:(i + 1) * 128], in_=xin4[:, t, c * 128:(c + 1) * 128], identity=ident32),
                                  reads=[xin4, cstf], writes=[pb])
                        evac(hT32, hT32[:, g0:g0 + 4, :], pb, pb[:, :].rearrange("p (c n) -> p c n", n=128))
                    pr = PB[0]
                    for c in range(KC):
                        mm(pr, pr[:, 0:E], hT32, hT32[:, c, :], wr, wr[:, c, :], c == 0, c == KC - 1)
                    sc, selb, mask, pos, oh, tmp = (rt[k] for k in ("sc", "selb", "mask", "pos", "oh", "tmp"))
                    cx.op("act", lambda e: e.activation(out=sc[:], in_=pr[:, 0:E], func=AF.Sigmoid), reads=[pr], writes=[sc])
                    cx.op("dve", lambda e: e.tensor_tensor(out=selb[:], in0=sc[:], in1=rbias[:], op=ALU.add), reads=[sc, rbias], writes=[selb])
                    cx.op("dve", lambda e: e.max(out=m8[:], in_=selb[:]), reads=[selb], writes=[m8])
                    cx.op("dve", lambda e: e.tensor_scalar(out=mask[:], in0=selb[:], scalar1=m8[:, TOPK - 1:TOPK], scalar2=None, op0=ALU.is_ge), reads=[selb, m8], writes=[mask])
                    pp = PB[1]
                    mm(pp, pp[:, 0:E], cstf, Us32, mask, mask[:], True, True)
                    mm(pp, pp[:, 128:128 + E], cstf, ones32, mask, mask[:], True, True)
                    cx.op("dve", lambda e: e.tensor_tensor(out=pos[:], in0=pp[:, 0:E], in1=rcar[:], op=ALU.add), reads=[pp, rcar], writes=[pos])
                    cx.op("dve", lambda e: e.tensor_tensor(out=rcar[:], in0=pp[:, 128:128 + E], in1=rcar[:], op=ALU.add), reads=[pp, rcar], writes=[rcar])
                    for k in range(TOPK):
                        cx.op("dve", lambda e, k=k: e.tensor_scalar(out=oh[:], in0=selb[:], scalar1=m8[:, k:k + 1], scalar2=None, op0=ALU.is_equal), reads=[selb, m8], writes=[oh])
                        for qi, srcb in enumerate((sc, pos, None)):
                            sap = iotaE if srcb is None else srcb[:]
                            cx.op("dve", lambda e, sap=sap: e.tensor_tensor(out=tmp[:], in0=oh[:], in1=sap, op=ALU.mult), reads=[oh, cstf] + ([srcb] if srcb else []), writes=[tmp])
                            cx.op("dve", lambda e, qi=qi, k=k: e.reduce_sum(out=r3[:, qi, k:k + 1], in_=tmp[:], axis=AX.X), reads=[tmp], writes=[r3])
                    cx.op("dve", lambda e: e.reduce_sum(out=rsum[:, 0:1], in_=r3[:, 0, 0:TOPK], axis=AX.X), reads=[r3], writes=[rsum])
                    cx.op("dve", lambda e: e.reciprocal(out=rsum[:, 1:2], in_=rsum[:, 0:1]), reads=[rsum], writes=[rsum])
                    cx.op("dve", lambda e, tile_i=tile_i: e.tensor_scalar(out=gates[:, tile_i, 0:TOPK], in0=r3[:, 0, 0:TOPK], scalar1=rsum[:, 1:2], scalar2=None, op0=ALU.mult),
                          reads=[r3, rsum], writes=[gates])
                    cx.op("dve", lambda e: e.tensor_scalar(out=r3[:, 1, 0:TOPK], in0=r3[:, 1, 0:TOPK], scalar1=float(C - 1), scalar2=None, op0=ALU.min), reads=[r3], writes=[r3])
                    cx.op("dve", lambda e: e.scalar_tensor_tensor(out=slf[:, 0:TOPK], in0=r3[:, 2, 0:TOPK], scalar=float(C), in1=r3[:, 1, 0:TOPK], op0=ALU.mult, op1=ALU.add),
                          reads=[r3], writes=[slf])
                    cx.op("dve", lambda e, tile_i=tile_i: e.tensor_copy(out=slots[:, tile_i, 0:TOPK], in_=slf[:, 0:TOPK]), reads=[slf], writes=[slots])

            for j in range(D // 256):
                def load(j=j):
                    st["wo", j] = rslot(KC, 256)
                    b, v = st["wo", j]
                    wload(b, v, wo16, kview(wo16, 0, KC, j * 256, 256))

                def comp(j=j):
                    b, v = st["wo", j]
                    for t in range(TT):
                        pb = PB[t % 2]
                        for c in range(KC):
                            mm(pb, pb[:, 0:256], mg, mg[:, c, t * 128:(t + 1) * 128], b, v[:, c, :], c == 0, c == KC - 1)
                        cx.op("dve", lambda e, t=t, pb=pb, j=j: e.scalar_tensor_tensor(
                            out=xin4[:, t, j * 256:(j + 1) * 256], in0=xin4[:, t, j * 256:(j + 1) * 256], scalar=ALPHA, in1=pb[:, 0:256], op0=ALU.mult, op1=ALU.add),
                            reads=[xin4, pb], writes=[xin4])
                    if j == D // 256 - 1:
                        ln_router()
                add_unit(load, comp)

            for j in range(FS // 256):
                def load(j=j):
                    st["s1", j] = rslot(KC, 256); b, v = st["s1", j]
                    wload(b, v, ws1b, kview(ws1b, 0, KC, j * 256, 256))
                    st["s3", j] = rslot(KC, 256); b, v = st["s3", j]
                    wload(b, v, ws3b, kview(ws3b, 0, KC, j * 256, 256))

                def comp(j=j):
                    b1, v1 = st["s1", j]; b3, v3 = st["s3", j]
                    for oc in range(2):
                        p1 = PB[oc]; p3 = PB[2 + oc]; s = sg[oc]
                        for c in range(KC):
                            mm(p1, p1[:, 0:TB], b1, v1[:, c, oc * 128:(oc + 1) * 128], hT, hT[:, c, :], c == 0, c == KC - 1)
                        for c in range(KC):
                            mm(p3, p3[:, 0:TB], b3, v3[:, c, oc * 128:(oc + 1) * 128], hT, hT[:, c, :], c == 0, c == KC - 1)
                        cx.op("act", lambda e, s=s, p1=p1: e.activation(out=s[:], in_=p1[:, 0:TB], func=AF.Silu), reads=[p1], writes=[s])
                        cx.op("dve", lambda e, s=s, p3=p3, ch=2 * j + oc: e.tensor_tensor(out=aT[:, ch, :], in0=p3[:, 0:TB], in1=s[:], op=ALU.mult), reads=[p3, s], writes=[aT])
                add_unit(load, comp)

            for j in range(D // 256):
                def load(j=j):
                    st["s2", j] = rslot(FSC, 256); b, v = st["s2", j]
                    wload(b, v, ws2b, kview(ws2b, 0, FSC, j * 256, 256))

                def comp(j=j):
                    b, v = st["s2", j]
                    for t in range(TT):
                        pb = PB[4 + t % 2]
                        for c in range(FSC):
                            mm(pb, pb[:, 0:256], aT, aT[:, c, t * 128:(t + 1) * 128], b, v[:, c, :], c == 0, c == FSC - 1)
                        cx.op("dve", lambda e, t=t, pb=pb, j=j: e.scalar_tensor_tensor(
                            out=xin4[:, t, j * 256:(j + 1) * 256], in0=xin4[:, t, j * 256:(j + 1) * 256], scalar=ALPHA, in1=pb[:, 0:256], op0=ALU.mult, op1=ALU.add),
                            reads=[xin4, pb], writes=[xin4])
                    if j == D // 256 - 1:
                        for t in range(TT):
                            tok0 = o0 + t * 128
                            dma(sp, Base, Base.t[tok0:tok0 + 128, :], xin4, xin4[:, t, :], sem=xin4)
                            for k in range(TOPK):
                                ti_ = tok0 // 128
                                cx.dma(pool, lambda e, ti_=ti_, k=k: e.indirect_dma_start(
                                    out=Tab.t[:, :], out_offset=bass.IndirectOffsetOnAxis(ap=slots[:, ti_, k:k + 1], axis=0),
                                    in_=tokid[:, ti_:ti_ + 1], in_offset=None), reads=[slots, tokid, Tab], writes=[], sem=slots)
                add_unit(load, comp)

        def run_units(depth):
            n = len(units)
            for i in range(n + depth):
                if i < n:
                    units[i][0]()
                if i >= depth:
                    units[i - depth][1]()

        for bk in range(NB):
            block_units(bk)
        run_units(1)

        cx.phase()
        units.clear()
        CT = C // 128
        RS5 = max(KC, FC) * 256
        ring5 = [cx.alloc("r5_%d" % i, [128, RS5], BF16) for i in range(6)]
        r5 = {"n": 0}

        def rslot5(nch, ncols):
            b = ring5[r5["n"] % len(ring5)]; r5["n"] += 1
            return b, b[:, 0:nch * ncols].rearrange("p (c n) -> p c n", n=ncols)

        tokt = [cx.alloc("tokt%d" % i, [128, CT], I32) for i in range(2)]
        xg = [cx.alloc("xg%d" % i, [128, D], BF16) for i in range(4)]
        XeT = [cx.alloc("XeT%d" % i, [128, KC, C], BF16) for i in range(2)]
        aTe = cx.alloc("aTe", [128, FC, C], BF16)
        st5 = [cx.alloc("st5_%d" % i, [128, C], F32) for i in range(2)]
        yst = [cx.alloc("yst%d" % i, [128, CT, 256], F32) for i in range(3)]
        gcnt = {"n": 0, "y": 0}

        def expert_units(ex):
            st = {}
            xe = XeT[ex % 2]

            def gather():
                tk = tokt[ex % 2]
                for i in range(CT):
                    dma(sp, tk, tk[:, i:i + 1], Tab, Tab.t[ex * C + i * 128:ex * C + (i + 1) * 128, :])
                for i in range(CT):
                    g = xg[gcnt["n"] % 4]; gcnt["n"] += 1
                    cx.dma(pool, lambda e, g=g, tk=tk, i=i: e.indirect_dma_start(
                        out=g[:], out_offset=None, in_=H16.t[:, :], in_offset=bass.IndirectOffsetOnAxis(ap=tk[:, i:i + 1], axis=0)),
                        reads=[tk, H16], writes=[g])
                    transpose_rows(g, g[:], 128, KC, xe, lambda c0, n, i=i: xe[:, c0:c0 + n, i * 128:(i + 1) * 128])

            n13 = (F + 255) // 256
            for j in range(n13):
                ncols = min(256, F - j * 256)

                def load(j=j, ncols=ncols):
                    if j == 0:
                        gather()
                    st["w1", j] = rslot5(KC, ncols); b, v = st["w1", j]
                    wload(b, v, w1, w1.t[ex, :, j * 256:j * 256 + ncols].rearrange("(c p) n -> p c n", p=128))
                    st["w3", j] = rslot5(KC, ncols); b, v = st["w3", j]
                    wload(b, v, w3, w3.t[ex, :, j * 256:j * 256 + ncols].rearrange("(c p) n -> p c n", p=128))

                def comp(j=j, ncols=ncols):
                    b1, v1 = st["w1", j]; b3, v3 = st["w3", j]
                    for oc in range(ncols // 128):
                        p1 = PB[oc]; p3 = PB[2 + oc]; s = st5[oc]
                        for c in range(KC):
                            mm(p1, p1[:, 0:C], b1, v1[:, c, oc * 128:(oc + 1) * 128], xe, xe[:, c, :], c == 0, c == KC - 1)
                        for c in range(KC):
                            mm(p3, p3[:, 0:C], b3, v3[:, c, oc * 128:(oc + 1) * 128], xe, xe[:, c, :], c == 0, c == KC - 1)
                        cx.op("act", lambda e, s=s, p1=p1: e.activation(out=s[:], in_=p1[:, 0:C], func=AF.Silu), reads=[p1], writes=[s])
                        cx.op("dve", lambda e, s=s, p3=p3, ch=2 * j + oc: e.tensor_tensor(out=aTe[:, ch, :], in0=p3[:, 0:C], in1=s[:], op=ALU.mult), reads=[p3, s], writes=[aTe])
                add_unit(load, comp)

            for j in range(D // 256):
                def load(j=j):
                    st["w2", j] = rslot5(FC, 256); b, v = st["w2", j]
                    wload(b, v, w2, w2.t[ex, :, j * 256:(j + 1) * 256].rearrange("(c p) n -> p c n", p=128))

                def comp(j=j):
                    b, v = st["w2", j]
                    ys = yst[gcnt["y"] % 3]; gcnt["y"] += 1
                    for i in range(CT):
                        pb = PB[4 + i % 2]
                        for c in range(FC):
                            mm(pb, pb[:, 0:256], aTe, aTe[:, c, i * 128:(i + 1) * 128], b, v[:, c, :], c == 0, c == FC - 1)
                        evac(ys, ys[:, i, :], pb, pb[:, 0:256])
                    dma(sp, Yb, Yb.t[ex * C:(ex + 1) * C, j * 256:(j + 1) * 256].rearrange("(i p) d -> p i d", p=128), ys, ys[:], sem=ys)
                add_unit(load, comp)

        for ex in range(E):
            expert_units(ex)
        run_units(2)

        cx.phase()
        ln2g = cx.alloc("ln2g", [128, D], F32); ln2b = cx.alloc("ln2b", [128, D], F32)
        dma(sp, ln2g, ln2g[:], ln2g_d, ln2g_d[:, :]); dma(sp, ln2b, ln2b[:], ln2b_d, ln2b_d[:, :])
        ln2 = layernorm(None, None, ln2g, ln2b, "2")
        acc = [cx.alloc("acc%d" % i, [128, D], F32) for i in range(2)]
        yk = [cx.alloc("yk%d" % i, [128, D], F32) for i in range(8)]
        ykc = 0
        for t in range(NT):
            a = acc[t % 2]
            dma(sp, a, a[:], Base, Base.t[t * 128:(t + 1) * 128, :])
            for k in range(TOPK):
                y = yk[ykc % 8]; ykc += 1
                cx.dma(pool, lambda e, y=y, t=t, k=k: e.indirect_dma_start(
                    out=y[:], out_offset=None, in_=Yb.t[:, :], in_offset=bass.IndirectOffsetOnAxis(ap=slots[:, t, k:k + 1], axis=0)),
                    reads=[slots, Yb], writes=[y])
                cx.op("dve", lambda e, y=y, a=a, t=t, k=k: e.scalar_tensor_tensor(
                    out=a[:], in0=y[:], scalar=gates[:, t, k:k + 1], in1=a[:], op0=ALU.mult, op1=ALU.add), reads=[y, a, gates], writes=[a])
            ln2(a, a[:])
            dma(sp, out_d, out_d.t[sh * TO + t * 128:sh * TO + (t + 1) * 128, :], a, a[:], sem=a)

    for sh in range(cfg.NSH):
        shard_phases(sh)

    cx.emit()
    return nc


def make_consts(cfg):
    E = cfg.E
    k = np.arange(128)[:, None]; m = np.arange(128)[None, :]
    cst = np.zeros((128, 5 * 128 + E), np.float32)
    cst[:, 0:128] = np.eye(128)
    cst[:, 128:256] = (k <= m)
    cst[:, 256:384] = (k < m)
    cst[:, 384:512] = 1.0
    cst[0, 512:640] = 1.0
    cst[:, 640:640 + E] = np.arange(E)[None, :]
    tokid = (np.arange(cfg.NT)[None, :] * 128 + np.arange(128)[:, None]).astype(np.int32)
    return cst, tokid


def prep_inputs(cfg, inp, n_cores=2):
    D, TO, NV, H, E = cfg.D, cfg.TO, cfg.NV, cfg.H, cfg.E
    x = np.asarray(inp["x"], np.float32)
    cst, tokid = make_consts(cfg)

    def rep(v, n):
        return np.ascontiguousarray(np.broadcast_to(np.asarray(v, np.float32).reshape(1, n), (128, n)))
    shared = {
        "cst": cst, "tokid": tokid,
        "w_in": np.ascontiguousarray(inp["w_in"][0]), "bfor": rep(inp["b_forget"][0], H),
        "pool_w": np.ascontiguousarray(inp["pool_w"][0]),
        "pscale": np.ascontiguousarray(np.asarray(inp["pool_scale"][0], np.float32).reshape(cfg.PWC, 128).T),
        "wbp": np.ascontiguousarray(inp["w_branch_pool"][0]), "wba": np.ascontiguousarray(inp["w_branch_attn"][0]),
        "w_out": np.ascontiguousarray(inp["w_out"][0]), "ln1g": rep(inp["ln1_g"][0], D), "ln1b": rep(inp["ln1_b"][0], D),
        "w_router": np.ascontiguousarray(inp["w_router"][0]), "rbias": rep(inp["router_bias"][0], E),
        "w1": np.ascontiguousarray(inp["w1"][0]), "w3": np.ascontiguousarray(inp["w3"][0]), "w2": np.ascontiguousarray(inp["w2"][0]),
        "ws1": np.ascontiguousarray(inp["w_shared1"][0]), "ws3": np.ascontiguousarray(inp["w_shared3"][0]),
        "ws2": np.ascontiguousarray(inp["w_shared2"][0]), "ln2g": rep(inp["ln2_g"][0], D), "ln2b": rep(inp["ln2_b"][0], D),
    }
    maps = []
    kbias = np.zeros((128, cfg.NKT), np.float32)
    corr = np.zeros((128, 64), np.float32)
    for g, w in enumerate(WINDOWS):
        corr[:, g * 16:(g + 1) * 16] = (1.0 / np.minimum(np.arange(16) + 1, w))[None, :]
    for c in range(n_cores):
        m = dict(shared)
        m.update({"xv": np.ascontiguousarray(x[c]), "kbias": kbias, "corr": corr})
        maps.append(m)
    return maps


_NC_CACHE = {}


def run(cfg, inp):
    key = (cfg.D, cfg.TO, cfg.E, cfg.TOPK, cfg.F, cfg.FS, cfg.C, cfg.TB, cfg.NSH)
    if key not in _NC_CACHE:
        _NC_CACHE[key] = build(cfg)
    nc = _NC_CACHE[key]
    maps = prep_inputs(cfg, inp)
    res = run_bass_kernel_spmd(nc, maps, core_ids=list(range(len(maps))))
    return np.stack([np.asarray(r["out"], np.float32) for r in res.results], axis=0)


def kernel(**inputs):
    return run(Cfg(), inputs)
```

```python
from contextlib import ExitStack
import numpy as np
import concourse.bass as bass
import concourse.mybir as mybir
from concourse.bass_utils import run_bass_kernel_spmd

F32 = mybir.dt.float32
BF16 = mybir.dt.bfloat16
I32 = mybir.dt.int32
AF = mybir.ActivationFunctionType
ALU = mybir.AluOpType
AX = mybir.AxisListType
WINDOWS = (2, 4, 8, 16)
LN_EPS = 1e-5
ALPHA = 2.0 ** 0.25
NEG = -30000.0


class Cfg:
    def __init__(self, D=2048, TO=4096, E=64, TOPK=6, F=1408, FS=2816, C=512, TB=256, NSH=4, SH0=2):
        self.D, self.TO, self.E, self.TOPK, self.F, self.FS, self.C, self.TB = D, TO, E, TOPK, F, FS, C, TB
        self.NSH = NSH
        self.SH0 = SH0
        self.NCORES = 2 * NSH // (NSH - SH0)
        self.NV = self.NSH * TO
        self.KC = D // 128
        self.PW = D // 2
        self.PWC = self.PW // 128
        self.GD = self.PW // 4
        self.GDC = self.GD // 128
        self.H = (D // 2) // 128
        self.AW = self.H * 128
        self.OFF_Q = self.PW
        self.OFF_K = self.OFF_Q + self.AW
        self.OFF_V = self.OFF_K + self.AW
        self.OFF_F = self.OFF_V + self.AW
        self.OFF_G = self.OFF_F + self.H
        self.INW = self.OFF_G + 2 * D
        self.FC = F // 128
        self.FSC = FS // 128
        self.NKT = self.NV // 128
        self.NT = TO // 128
        self.NQS = TO // 256


class Buf:
    __slots__ = ("name", "t", "last_w", "readers", "dsem", "track")

    def __init__(self, name, t=None, track=True):
        self.name = name
        self.t = t
        self.last_w = None
        self.readers = {}
        self.dsem = None
        self.track = track

    def __getitem__(self, k):
        return self.t[k]


class Ctx:
    ENG = ("pe", "act", "dve", "pool", "sp")
    ARENA = 103000

    def __init__(self, nc):
        self.nc = nc
        self.es = ExitStack()
        self.q = {e: [] for e in self.ENG}
        self.cnt = {e: 0 for e in self.ENG}
        self.dcnt = {}
        self.ndsem = 0
        self.arena = self.es.enter_context(nc.sbuf_tensor("arena", [128, self.ARENA], BF16))
        self.ptr = 0
        self.mark = 0

    def alloc(self, name, shape, dt):
        n = int(np.prod(shape[1:]))
        nb = n * (2 if dt != BF16 else 1)
        self.ptr = (self.ptr + 15) // 16 * 16
        assert self.ptr + nb <= self.ARENA, (name, self.ptr, nb)
        v = self.arena[0:shape[0], self.ptr:self.ptr + nb]
        self.ptr += nb
        if dt != BF16:
            v = v.bitcast(dt)
        if len(shape) == 3:
            v = v.rearrange("p (a b) -> p a b", a=shape[1])
        elif len(shape) == 4:
            v = v.rearrange("p (a b c) -> p a b c", a=shape[1], b=shape[2])
        return Buf(name, v)

    def ps(self, name, shape, dt=F32):
        t = self.es.enter_context(self.nc.psum_tensor(name, list(shape), dt))
        return Buf(name, t)

    def phase(self):
        self.barrier()
        self.ptr = self.mark
        self.ndsem = self.ndsem_mark

    def keep(self):
        self.mark = self.ptr
        self.ndsem_mark = self.ndsem

    def barrier(self):
        allw = [(("c", e), v) for e, v in self.cnt.items() if v] + [(("d", n), v) for n, v in self.dcnt.items()]
        for e in self.ENG:
            self.q[e].append((list(allw), None, None, 0))

    def _deps(self, eng, reads, writes):
        w = {}

        def add(tok):
            if tok is None:
                return
            k, v = tok
            if k[0] == "c" and k[1] == eng and eng == "pe":
                return
            if k[0] == "d":
                v = self.dcnt[k[1]]
            if w.get(k, 0) < v:
                w[k] = v
        for r in reads:
            add(r.last_w)
        for b in writes:
            add(b.last_w)
            for k, v in b.readers.items():
                add((k, v))
        return list(w.items())

    def _commit(self, tok, reads, writes):
        k, v = tok
        for r in reads:
            if r.readers.get(k, 0) < v:
                r.readers[k] = v
        for b in writes:
            b.last_w = tok
            b.readers = {}

    def op(self, eng, fn, reads=(), writes=()):
        waits = self._deps(eng, reads, writes)
        self.cnt[eng] += 1
        tok = (("c", eng), self.cnt[eng])
        self.q[eng].append((waits, fn, tok[0], 1))
        self._commit(tok, reads, writes)

    def dma(self, eng, fn, reads=(), writes=(), sem=None):
        reads = [b for b in reads if b.track]
        writes = [b for b in writes if b.track]
        if sem is None:
            sem = (list(writes) + list(reads))[0]
        if sem.dsem is None:
            sem.dsem = self.ndsem
            self.ndsem += 1
        name = sem.dsem
        waits = self._deps(eng, reads, writes)
        self.dcnt[name] = self.dcnt.get(name, 0) + 16
        tok = (("d", name), self.dcnt[name])
        self.q[eng].append((waits, fn, tok[0], 16))
        self._commit(tok, reads, writes)

    def emit(self):
        nc = self.nc
        self.barrier()
        sems = {}
        for e in self.ENG:
            if self.cnt[e]:
                sems[("c", e)] = self.es.enter_context(nc.semaphore("c_" + e))
        for name in self.dcnt:
            sems[("d", name)] = self.es.enter_context(nc.semaphore("d_%d" % name))
        block = self.es.enter_context(nc.Block())

        def run(e, engobj):
            waited = {}
            for waits, fn, sk, amt in self.q[e]:
                for k, v in waits:
                    if waited.get(k, 0) < v:
                        engobj.wait_ge(sems[k], v)
                        waited[k] = v
                if fn is not None:
                    fn(engobj).then_inc(sems[sk], amt)

        block.tensor(lambda x: run("pe", x))
        block.scalar(lambda x: run("act", x))
        block.vector(lambda x: run("dve", x))
        block.gpsimd(lambda x: run("pool", x))
        block.sync(lambda x: run("sp", x))
        self.es.close()


def build(cfg):
    nc = bass.Bass("TRN2", target_bir_lowering=False)
    D, TO, E, TOPK, F, FS, C, TB = cfg.D, cfg.TO, cfg.E, cfg.TOPK, cfg.F, cfg.FS, cfg.C, cfg.TB
    NV, KC, PW, PWC, GD, GDC, H, AW = cfg.NV, cfg.KC, cfg.PW, cfg.PWC, cfg.GD, cfg.GDC, cfg.H, cfg.AW
    FC, FSC, NKT, NT, NQS = cfg.FC, cfg.FSC, cfg.NKT, cfg.NT, cfg.NQS
    NCST = 5 * 128 + E

    def ein(name, shape, dt=F32):
        return Buf(name, nc.dram_tensor(name, list(shape), dt, kind="ExternalInput").ap(), track=False)

    def scr(name, shape, dt):
        return Buf(name, nc.dram_tensor(name, list(shape), dt, kind="Internal").ap(), track=False)

    xv = ein("xv", [NV, D]); kbias_d = ein("kbias", [128, NKT]); corr_d = ein("corr", [128, 64])
    cst_d = ein("cst", [128, NCST]); tokid_d = ein("tokid", [128, NT], I32)
    w_in = ein("w_in", [D, cfg.INW]); bfor_d = ein("bfor", [128, H]); pool_w = ein("pool_w", [4, GD, GD])
    pscale_d = ein("pscale", [128, PWC]); wbp = ein("wbp", [PW, D]); wba = ein("wba", [AW, D])
    w_out = ein("w_out", [D, D]); ln1g_d = ein("ln1g", [128, D]); ln1b_d = ein("ln1b", [128, D])
    w_router = ein("w_router", [D, E]); rbias_d = ein("rbias", [128, E])
    w1 = ein("w1", [E, D, F]); w3 = ein("w3", [E, D, F]); w2 = ein("w2", [E, F, D])
    ws1 = ein("ws1", [D, FS]); ws3 = ein("ws3", [D, FS]); ws2 = ein("ws2", [FS, D])
    ln2g_d = ein("ln2g", [128, D]); ln2b_d = ein("ln2b", [128, D])
    out_d = Buf("out", nc.dram_tensor("out", [(cfg.NSH - cfg.SH0) * TO, D], F32, kind="ExternalOutput").ap(), track=False)

    Ks = scr("Ks", [H, 128, NV], BF16); Vs = scr("Vs", [H, NV, 128], BF16)
    Qs = scr("Qs", [H, 128, NV], BF16); As = scr("As", [H, 128, NV], BF16)
    Gs = scr("Gs", [128, NKT * H], F32)
    H32 = scr("H32", [TO, D], F32); H16 = scr("H16", [TO, D], BF16); Base = scr("Base", [TO, D], F32)
    Tab = scr("Tab", [E * C, 1], I32); Tab.track = True; Yb = scr("Yb", [E * C, D], F32)

    cx = Ctx(nc)
    sp, pool = "sp", "pool"

    def dma(eng, out_b, out_ap, in_b, in_ap, sem=None, **kw):
        cx.dma(eng, lambda e: e.dma_start(out=out_ap, in_=in_ap, **kw), reads=[in_b], writes=[out_b], sem=sem)

    def mm(ob, oap, lb, lap, rb, rap, start, stop):
        cx.op("pe", lambda e: e.matmul(oap, lhsT=lap, rhs=rap, start=start, stop=stop), reads=[lb, rb], writes=[ob])

    rr = {"n": 0}

    def evac(ob, oap, ib, iap):
        rr["n"] += 1
        if rr["n"] % 2:
            cx.op("act", lambda e: e.copy(out=oap, in_=iap), reads=[ib], writes=[ob])
        else:
            cx.op("dve", lambda e: e.tensor_copy(out=oap, in_=iap), reads=[ib], writes=[ob])

    PB = [cx.ps("pb%d" % i, [128, 512], F32) for i in range(6)]
    TBK = [cx.ps("tb%d" % i, [128, 1024], BF16) for i in range(2)]
    tcnt = {"n": 0}

    cstf = cx.alloc("cstf", [128, NCST], F32)
    ident32 = cstf[:, 0:128]; U32 = cstf[:, 128:256]; Us32 = cstf[:, 256:384]; ones32 = cstf[:, 384:512]
    E032 = cstf[:, 512:640]; iotaE = cstf[:, 640:640 + E]
    cstb = cx.alloc("cstb", [128, 384], BF16)
    identb = cstb[:, 0:128]; trib = cstb[:, 128:256]; onesb = cstb[:, 256:384]
    tokid = cx.alloc("tokid", [128, NT], I32)
    gates = cx.alloc("gates", [128, NT, 8], F32)
    slots = cx.alloc("slots", [128, NT, 8], I32)
    dma(sp, cstf, cstf[:], cst_d, cst_d[:, :])
    dma(sp, tokid, tokid[:], tokid_d, tokid_d[:, :])
    cx.op("dve", lambda e: e.tensor_copy(out=cstb[:, 0:256], in_=cstf[:, 0:256]), reads=[cstf], writes=[cstb])
    cx.op("dve", lambda e: e.tensor_copy(out=cstb[:, 256:384], in_=cstf[:, 384:512]), reads=[cstf], writes=[cstb])
    cx.keep()

    def transpose_rows(src_b, src_ap, nrows, nchunks, dst_b, dst_fn, ident=None):
        for g0 in range(0, nchunks, 8):
            n = min(8, nchunks - g0)
            tb = TBK[tcnt["n"] % 2]; tcnt["n"] += 1
            for i in range(n):
                c = g0 + i
                cx.op("pe", lambda e, tb=tb, i=i, c=c: e.transpose(
                    out=tb[:, i * 128:i * 128 + nrows], in_=src_ap[:, c * 128:(c + 1) * 128],
                    identity=identb[0:nrows, 0:nrows]), reads=[src_b, cstb], writes=[tb])
            evac(dst_b, dst_fn(g0, n), tb, tb[:, 0:n * 128].rearrange("p (c n) -> p c n", n=128)[:, :, 0:nrows])

    def wload(slot, shape_ap, src_b, src_ap):
        dma(pool, slot, shape_ap, src_b, src_ap)

    def kview(wb, r0, nchunks, c0, ncols):
        return wb.t[r0:r0 + nchunks * 128, c0:c0 + ncols].rearrange("(c p) n -> p c n", p=128)

    def layernorm(xb, xap, gb, bb, tag):
        FM = min(D, int(nc.vector.BN_STATS_FMAX))
        nst = D // FM
        st = cx.alloc("lnst" + tag, [128, nst, int(nc.vector.BN_STATS_DIM)], F32)
        mv = cx.alloc("lnmv" + tag, [128, int(nc.vector.BN_AGGR_DIM)], F32)
        sd = cx.alloc("lnsd" + tag, [128, 2], F32)

        def run(xb, xap):
            for c in range(nst):
                cx.op("dve", lambda e, c=c: e.bn_stats(out=st[:, c, :], in_=xap[:, c * FM:(c + 1) * FM]), reads=[xb], writes=[st])
            cx.op("dve", lambda e: e.bn_aggr(out=mv[:], in_=st[:]), reads=[st], writes=[mv])
            cx.op("act", lambda e: e.activation(out=sd[:, 0:1], in_=mv[:, 1:2], func=AF.Sqrt, bias=LN_EPS, scale=1.0), reads=[mv], writes=[sd])
            cx.op("dve", lambda e: e.reciprocal(out=sd[:, 1:2], in_=sd[:, 0:1]), reads=[sd], writes=[sd])
            cx.op("dve", lambda e: e.tensor_scalar(out=xap, in0=xap, scalar1=mv[:, 0:1], scalar2=sd[:, 1:2],
                                                  op0=ALU.subtract, op1=ALU.mult), reads=[xb, mv, sd], writes=[xb])
            cx.op("dve", lambda e: e.tensor_tensor(out=xap, in0=xap, in1=gb[:], op=ALU.mult), reads=[xb, gb], writes=[xb])
            cx.op("dve", lambda e: e.tensor_tensor(out=xap, in0=xap, in1=bb[:], op=ALU.add), reads=[xb, bb], writes=[xb])
        return run

    def xpass(which):
        cx.phase()
        XB = 512
        if which == "kvf":
            ncol = 2 * AW + H
            wres = cx.alloc("wres", [128, KC, ncol], BF16)
            for c in range(KC):
                wload(wres, wres[:, c, 0:2 * AW], w_in, w_in.t[c * 128:(c + 1) * 128, cfg.OFF_K:cfg.OFF_K + 2 * AW])
                wload(wres, wres[:, c, 2 * AW:ncol], w_in, w_in.t[c * 128:(c + 1) * 128, cfg.OFF_F:cfg.OFF_F + H])
            blocks = range(NV // XB)
            row_base = 0
        else:
            wres = cx.alloc("wres", [128, KC, AW], BF16)
            for c in range(KC):
                wload(wres, wres[:, c, :], w_in, w_in.t[c * 128:(c + 1) * 128, cfg.OFF_Q:cfg.OFF_Q + AW])
            blocks = range(cfg.SH0 * TO // XB, NV // XB)
            row_base = 0
        xin = [cx.alloc("xin%d" % i, [128, D], F32) for i in range(3)]
        xbf = [cx.alloc("xbf%d" % i, [128, D], BF16) for i in range(2)]
        xT = [cx.alloc("xT%d" % i, [128, KC, XB], BF16) for i in range(2)]
        kst = [cx.alloc("kst%d" % i, [128, H, XB], BF16) for i in range(2)]
        if which == "kvf":
            vst = [cx.alloc("vst%d" % i, [128, XB // 128, AW], BF16) for i in range(2)]
            bfor = cx.alloc("bfor", [128, H], F32)
            dma(sp, bfor, bfor[:], bfor_d, bfor_d[:, :])
            Gtok = cx.alloc("Gtok", [128, NKT, H], F32)
            carry = cx.alloc("carry", [128, H], F32)
            cx.op("dve", lambda e: e.memset(carry[:], 0.0), writes=[carry])
            fb = [cx.alloc("fb%d" % i, [128, H], F32) for i in range(2)]
        ti = 0
        for bi, blk in enumerate(blocks):
            r0 = row_base + blk * XB
            xt = xT[bi % 2]
            for t in range(XB // 128):
                xi = xin[ti % 3]; xb = xbf[ti % 2]; ti += 1
                dma(sp, xi, xi[:], xv, xv.t[r0 + t * 128:r0 + (t + 1) * 128, :])
                if ti % 2:
                    cx.op("act", lambda e, xi=xi, xb=xb: e.copy(out=xb[:], in_=xi[:]), reads=[xi], writes=[xb])
                else:
                    cx.op("dve", lambda e, xi=xi, xb=xb: e.tensor_copy(out=xb[:], in_=xi[:]), reads=[xi], writes=[xb])
                transpose_rows(xb, xb[:], 128, KC, xt, lambda c0, n, xt=xt, t=t: xt[:, c0:c0 + n, t * 128:(t + 1) * 128])
            ks = kst[bi % 2]
            for h in range(H):
                pb = PB[h % 2]
                for c in range(KC):
                    mm(pb, pb[:, :], wres, wres[:, c, h * 128:(h + 1) * 128], xt, xt[:, c, :], c == 0, c == KC - 1)
                if which == "q":
                    rr["n"] += 1
                    cx.op("act", lambda e, ks=ks, h=h, pb=pb: e.mul(out=ks[:, h, :], in_=pb[:, :], mul=128.0 ** -0.5), reads=[pb], writes=[ks])
                else:
                    evac(ks, ks[:, h, :], pb, pb[:, :])
            if which == "q":
                dma(sp, Qs, Qs.t[:, :, blk * XB:(blk + 1) * XB].rearrange("h d t -> d h t"), ks, ks[:], sem=ks)
                continue
            dma(sp, Ks, Ks.t[:, :, r0:r0 + XB].rearrange("h d t -> d h t"), ks, ks[:], sem=ks)
            vs = vst[bi % 2]
            for t in range(XB // 128):
                for nb in range(0, AW, 512):
                    pb = PB[2 + (t + nb // 512) % 2]
                    for c in range(KC):
                        mm(pb, pb[:, :], xt, xt[:, c, t * 128:(t + 1) * 128], wres, wres[:, c, AW + nb:AW + nb + 512], c == 0, c == KC - 1)
                    evac(vs, vs[:, t, nb:nb + 512], pb, pb[:, :])
            for t in range(XB // 128):
                dma(sp, Vs, Vs.t[:, r0 + t * 128:r0 + (t + 1) * 128, :].rearrange("h p d -> p h d"), vs,
                    vs[:, t, :].rearrange("p (h d) -> p h d", d=128), sem=vs)
            for t in range(XB // 128):
                kt = (r0 // 128) + t
                pf = PB[4]; pg = PB[5]; f = fb[t % 2]
                for c in range(KC):
                    mm(pf, pf[:, 0:H], xt, xt[:, c, t * 128:(t + 1) * 128], wres, wres[:, c, 2 * AW:2 * AW + H], c == 0, c == KC - 1)
                cx.op("dve", lambda e, f=f, pf=pf: e.tensor_tensor(out=f[:], in0=pf[:, 0:H], in1=bfor[:], op=ALU.add), reads=[pf, bfor], writes=[f])
                cx.op("act", lambda e, f=f: e.activation(out=f[:], in_=f[:], func=AF.Exp, scale=-1.0), reads=[f], writes=[f])
                cx.op("act", lambda e, f=f: e.activation(out=f[:], in_=f[:], func=AF.Ln, bias=1.0, scale=1.0), reads=[f], writes=[f])
                mm(pg, pg[:, 0:H], cstf, U32, f, f[:], True, True)
                mm(pg, pg[:, 64:64 + H], cstf, ones32, f, f[:], True, True)
                cx.op("dve", lambda e, kt=kt, pg=pg: e.tensor_tensor(out=Gtok[:, kt, :], in0=pg[:, 0:H], in1=carry[:], op=ALU.add), reads=[pg, carry], writes=[Gtok])
                cx.op("dve", lambda e, pg=pg: e.tensor_tensor(out=carry[:], in0=pg[:, 64:64 + H], in1=carry[:], op=ALU.add), reads=[pg, carry], writes=[carry])
        if which == "kvf":
            dma(sp, Gs, Gs[:, :], Gtok, Gtok[:].rearrange("p a b -> p (a b)"), sem=Gtok)

    Wi16 = scr("Wi16", [D, PW + 2 * D], BF16); wbp16 = scr("wbp16", [PW, D], BF16); wba16 = scr("wba16", [AW, D], BF16)
    wo16 = scr("wo16", [D, D], BF16); ws1b = scr("ws1b", [D, FS], BF16); ws3b = scr("ws3b", [D, FS], BF16); ws2b = scr("ws2b", [FS, D], BF16)
    cx.phase()
    bnc = [cx.alloc("bnc%d" % i, [128, PW + 2 * D], BF16) for i in range(3)]
    bstate = {"n": 0}

    def precast(dst, src, nrows, segs):
        for r in range(0, nrows, 128):
            b = bnc[bstate["n"] % 3]; bstate["n"] += 1
            tot = 0
            for (sc, n, dc) in segs:
                dma(pool, b, b[:, dc:dc + n], src, src.t[r:r + 128, sc:sc + n])
                tot = max(tot, dc + n)
            dma(sp, dst, dst.t[r:r + 128, 0:tot], b, b[:, 0:tot], sem=b)

    precast(Wi16, w_in, D, [(0, PW, 0), (cfg.OFF_G, 2 * D, PW)])
    precast(wbp16, wbp, PW, [(0, D, 0)]); precast(wba16, wba, AW, [(0, D, 0)]); precast(wo16, w_out, D, [(0, D, 0)])
    precast(ws1b, ws1, D, [(0, FS, 0)]); precast(ws3b, ws3, D, [(0, FS, 0)]); precast(ws2b, ws2, FS, [(0, D, 0)])

    xpass("kvf")
    xpass("q")

    def shard_phases(sh):
        cx.phase()
        Gtok = cx.alloc("Gtok3", [128, NKT, H], F32)
        kb = cx.alloc("kb3", [128, NKT], F32)
        dma(sp, Gtok, Gtok[:].rearrange("p a b -> p (a b)"), Gs, Gs[:, :])
        dma(sp, kb, kb[:], kbias_d, kbias_d[:, :])
        KT = [cx.alloc("KT%d" % i, [128, NV], BF16) for i in range(2)]
        VH = [cx.alloc("VH%d" % i, [128, NKT, 128], BF16) for i in range(2)]
        QT = [cx.alloc("QT%d" % i, [128, TO], BF16) for i in range(2)]
        BT = [cx.alloc("BT%d" % i, [128, NKT, NQS], F32) for i in range(2)]
        gk = cx.alloc("gk", [128, NKT], F32)
        gm = cx.alloc("gm", [128, NQS], F32)
        Pt = [cx.alloc("P%d" % i, [128, 512], BF16) for i in range(4)]
        rec = cx.alloc("rec", [128, 512], F32)
        ost = [cx.alloc("ost%d" % i, [128, 512], BF16) for i in range(2)]
        KT0 = sh * TO // 128
        pcnt = 0
        qbcnt = 0

        def load_head(h):
            dma(sp, KT[h % 2], KT[h % 2][:], Ks, Ks.t[h, :, :])
            for t0 in range(0, NKT, 16):
                n = min(16, NKT - t0)
                dma(sp, VH[h % 2], VH[h % 2][:, t0:t0 + n, :], Vs,
                    Vs.t[h, t0 * 128:(t0 + n) * 128, :].rearrange("(t p) d -> p t d", p=128))
            dma(sp, QT[h % 2], QT[h % 2][:], Qs, Qs.t[h, :, sh * TO:(sh + 1) * TO])

        load_head(0)
        for h in range(H):
            if h + 1 < H:
                load_head(h + 1)
            kt_, vh_, qt_, bt_ = KT[h % 2], VH[h % 2], QT[h % 2], BT[h % 2]
            pgm = PB[4]
            mids = Gtok[:, KT0:KT0 + 2 * NQS, h].rearrange("p (q two) -> p q two", two=2)[:, :, 1]
            cx.op("dve", lambda e, h=h: e.tensor_tensor(out=gk[:], in0=Gtok[:, :, h], in1=kb[:], op=ALU.add), reads=[Gtok, kb], writes=[gk])
            cx.op("dve", lambda e, mids=mids: e.tensor_copy(out=gm[:], in_=mids), reads=[Gtok], writes=[gm])
            mm(pgm, pgm[:, 0:NQS], cstf, E032, gm, gm[:], True, True)
            cx.op("act", lambda e: e.copy(out=gm[:], in_=pgm[:, 0:NQS]), reads=[pgm], writes=[gm])
            cx.op("dve", lambda e, bt_=bt_: e.tensor_tensor(
                out=bt_[:], in0=gk[:].unsqueeze(2).to_broadcast([128, NKT, NQS]),
                in1=gm[:].unsqueeze(1).to_broadcast([128, NKT, NQS]), op=ALU.subtract), reads=[gk, gm], writes=[bt_])
            for qb in range(TO // 512):
                qbcnt += 1
                po = PB[qbcnt % 2]; pl = PB[4 + qbcnt % 2]
                nk = KT0 + 4 * qb + 4
                items = []
                for kt in range(nk):
                    col0 = max(0, kt - (KT0 + 4 * qb)) * 128
                    items.append((kt, col0))

                def emit_s(i):
                    kt, col0 = items[i]
                    pss = PB[2 + i % 2]
                    mm(pss, pss[:, col0:512], kt_, kt_[:, kt * 128:(kt + 1) * 128], qt_, qt_[:, qb * 512 + col0:(qb + 1) * 512], True, True)

                def emit_pv(i):
                    nonlocal pcnt
                    kt, col0 = items[i]
                    pss = PB[2 + i % 2]
                    p = Pt[pcnt % 4]; pcnt += 1
                    for half in range(2):
                        a = max(col0, half * 256); b = (half + 1) * 256
                        if a >= b:
                            continue
                        bap = bt_[:, kt, 2 * qb + half:2 * qb + half + 1]
                        cx.op("act", lambda e, a=a, b=b, p=p, pss=pss, bap=bap: e.activation(
                            out=p[:, a:b], in_=pss[:, a:b], func=AF.Exp, bias=bap, scale=1.0),
                            reads=[pss, bt_], writes=[p])
                    if kt >= KT0 + 4 * qb:
                        cx.op("pool", lambda e, p=p, col0=col0: e.tensor_tensor(out=p[:, col0:col0 + 128], in0=p[:, col0:col0 + 128], in1=trib, op=ALU.mult),
                              reads=[p, cstb], writes=[p])
                    mm(po, po[:, col0:512], vh_, vh_[:, kt, :], p, p[:, col0:512], i == 0, i == len(items) - 1)
                    mm(pl, pl[:, col0:512], cstb, onesb, p, p[:, col0:512], i == 0, i == len(items) - 1)

                for i in range(len(items) + 1):
                    if i < len(items):
                        emit_s(i)
                    if i >= 1:
                        emit_pv(i - 1)
                o = ost[qb % 2]
                cx.op("dve", lambda e, pl=pl: e.reciprocal(out=rec[:], in_=pl[:, :]), reads=[pl], writes=[rec])
                cx.op("dve", lambda e, o=o, po=po: e.tensor_tensor(out=o[:], in0=po[:, :], in1=rec[:], op=ALU.mult), reads=[po, rec], writes=[o])
                dma(sp, As, As.t[h, :, sh * TO + qb * 512:sh * TO + (qb + 1) * 512], o, o[:], sem=o)

        cx.phase()
        NB = TO // TB
        TT = TB // 128
        XW = TB + 16
        RS = max(KC, FSC, PWC, H) * 256
        ring = [cx.alloc("ring%d" % i, [128, RS], BF16) for i in range(4)]
        rcnt = {"n": 0}

        def rslot(nch, ncols):
            b = ring[rcnt["n"] % len(ring)]; rcnt["n"] += 1
            return b, b[:, 0:nch * ncols].rearrange("p (c n) -> p c n", n=ncols)

        xin4 = cx.alloc("xin4", [128, TT, D], F32)
        xinh = cx.alloc("xinh", [128, D], F32)
        xbf4 = cx.alloc("xbf4", [128, D], BF16)
        xTe = cx.alloc("xTe", [128, KC, XW], BF16)
        u = cx.alloc("u", [128, PWC, XW], F32)
        sA = cx.alloc("sA", [128, PWC, XW], F32)
        sB = cx.alloc("sB", [128, PWC, XW], F32)
        pT = cx.alloc("pT", [128, PWC, TB], BF16)
        ypT = cx.alloc("ypT", [128, PWC, TB], BF16)
        atT = cx.alloc("atT", [128, H, TB], BF16)
        mg = cx.alloc("mg", [128, KC, TB], BF16)
        sg = [cx.alloc("sg%d" % i, [128, TB], F32) for i in range(2)]
        tmpf = cx.alloc("tmpf", [128, TB], F32)
        hb = cx.alloc("hb", [128, D], BF16)
        hT = cx.alloc("hT", [128, KC, TB], BF16)
        hT32 = cx.alloc("hT32", [128, KC, 128], F32)
        aT = cx.alloc("aT", [128, FSC, TB], BF16)
        ln1g = cx.alloc("ln1g", [128, D], F32); ln1b = cx.alloc("ln1b", [128, D], F32)
        wr = cx.alloc("wr", [128, KC, E], F32)
        pw = cx.alloc("pw", [128, 4 * GDC, GD], BF16)
        psc = cx.alloc("psc", [128, PWC], F32)
        corr = cx.alloc("corr", [128, 4, 16], F32)
        rbias = cx.alloc("rbias", [128, E], F32)
        rcar = cx.alloc("rcar", [128, E], F32)
        rt = {k: cx.alloc("rt_" + k, [128, E], F32) for k in ("sc", "selb", "mask", "pos", "oh", "tmp")}
        m8 = cx.alloc("m8", [128, 8], F32)
        r3 = cx.alloc("r3", [128, 3, 8], F32)
        rsum = cx.alloc("rsum", [128, 2], F32)
        slf = cx.alloc("slf", [128, 8], F32)
        zt = cx.alloc("zt", [128, E * C // 128], I32)
        ln1 = layernorm(None, None, ln1g, ln1b, "1")
        dma(sp, ln1g, ln1g[:], ln1g_d, ln1g_d[:, :]); dma(sp, ln1b, ln1b[:], ln1b_d, ln1b_d[:, :])
        dma(sp, wr, wr[:], w_router, w_router.t[:, :].rearrange("(c p) e -> p c e", p=128))
        dma(sp, psc, psc[:], pscale_d, pscale_d[:, :])
        dma(sp, corr, corr[:].rearrange("p a b -> p (a b)"), corr_d, corr_d[:, :])
        dma(sp, rbias, rbias[:], rbias_d, rbias_d[:, :])
        for g in range(4):
            wload(pw, pw[:, g * GDC:(g + 1) * GDC, :], pool_w, pool_w.t[g, :, :].rearrange("(c p) n -> p c n", p=128))
        cx.op("dve", lambda e: e.memset(rcar[:], 0.0), writes=[rcar])
        cx.op("dve", lambda e: e.memset(zt[:], 0), writes=[zt])
        dma(sp, Tab, Tab.t[:, :].rearrange("(p i) o -> p (i o)", p=128), zt, zt[:], sem=zt)

        units = []

        def add_unit(load, comp):
            units.append((load, comp))

        def block_units(bk):
            r0 = sh * TO + bk * TB
            o0 = bk * TB
            st = {}

            def prologue():
                if r0 == 0:
                    cx.op("dve", lambda e: e.memset(xTe[:, :, 0:16], 0.0), writes=[xTe])
                else:
                    dma(sp, xinh, xinh[0:16, :], xv, xv.t[r0 - 16:r0, :])
                    cx.op("dve", lambda e: e.tensor_copy(out=xbf4[0:16, :], in_=xinh[0:16, :]), reads=[xinh], writes=[xbf4])
                    transpose_rows(xbf4, xbf4[0:16, :], 16, KC, xTe, lambda c0, n: xTe[:, c0:c0 + n, 0:16])
                for t in range(TT):
                    dma(sp, xin4, xin4[:, t, :], xv, xv.t[r0 + t * 128:r0 + (t + 1) * 128, :])
                    cx.op("act", lambda e, t=t: e.copy(out=xbf4[:], in_=xin4[:, t, :]), reads=[xin4], writes=[xbf4])
                    transpose_rows(xbf4, xbf4[:], 128, KC, xTe, lambda c0, n, t=t: xTe[:, c0:c0 + n, 16 + t * 128:16 + (t + 1) * 128])
                dma(sp, atT, atT[:], As, As.t[:, :, sh * TO + o0:sh * TO + o0 + TB].rearrange("h d t -> d h t"))

            for j in range(PW // 256):
                def load(j=j):
                    st["wp", j] = rslot(KC, 256)
                    b, v = st["wp", j]
                    wload(b, v, Wi16, kview(Wi16, 0, KC, j * 256, 256))

                def comp(j=j):
                    if j == 0:
                        prologue()
                    b, v = st["wp", j]
                    for oc in range(2):
                        pb = PB[oc]
                        for c in range(KC):
                            mm(pb, pb[:, 0:XW], b, v[:, c, oc * 128:(oc + 1) * 128], xTe, xTe[:, c, :], c == 0, c == KC - 1)
                        evac(u, u[:, 2 * j + oc, :], pb, pb[:, 0:XW])
                add_unit(load, comp)

            def pool_mix():
                cpg = GDC
                cx.op("dve", lambda e: e.tensor_tensor(out=sA[:, :, 1:XW], in0=u[:, :, 1:XW], in1=u[:, :, 0:XW - 1], op=ALU.add), reads=[u], writes=[sA])
                cx.op("dve", lambda e: e.tensor_tensor(out=sB[:, cpg:, 3:XW], in0=sA[:, cpg:, 3:XW], in1=sA[:, cpg:, 1:XW - 2], op=ALU.add), reads=[sA], writes=[sB])
                cx.op("dve", lambda e: e.tensor_tensor(out=sA[:, 2 * cpg:, 7:XW], in0=sB[:, 2 * cpg:, 7:XW], in1=sB[:, 2 * cpg:, 3:XW - 4], op=ALU.add), reads=[sB], writes=[sA])
                cx.op("dve", lambda e: e.tensor_tensor(out=sB[:, 3 * cpg:, 15:XW], in0=sA[:, 3 * cpg:, 15:XW], in1=sA[:, 3 * cpg:, 7:XW - 8], op=ALU.add), reads=[sA], writes=[sB])
                srcs = [sA, sB, sA, sB]
                for g in range(4):
                    s = srcs[g]; cs = slice(g * cpg, (g + 1) * cpg)
                    cx.op("dve", lambda e, s=s, cs=cs, g=g: e.scalar_tensor_tensor(
                        out=pT[:, cs, :], in0=s[:, cs, 16:XW], scalar=1.0 / WINDOWS[g], in1=u[:, cs, 16:XW], op0=ALU.mult, op1=ALU.subtract),
                        reads=[s, u], writes=[pT])
                    if bk == 0 and sh == cfg.SH0:
                        cx.op("dve", lambda e, s=s, cs=cs, g=g: e.tensor_tensor(
                            out=s[:, cs, 16:32], in0=s[:, cs, 16:32], in1=corr[:, g:g + 1, :].to_broadcast([128, cpg, 16]), op=ALU.mult),
                            reads=[s, corr], writes=[s])
                        cx.op("dve", lambda e, s=s, cs=cs: e.tensor_tensor(out=pT[:, cs, 0:16], in0=s[:, cs, 16:32], in1=u[:, cs, 16:32], op=ALU.subtract),
                              reads=[s, u], writes=[pT])
                for g in range(4):
                    for oc in range(GDC):
                        pb = PB[2 + (g * GDC + oc) % 2]
                        for ic in range(GDC):
                            mm(pb, pb[:, 0:TB], pw, pw[:, g * GDC + ic, oc * 128:(oc + 1) * 128], pT, pT[:, g * GDC + ic, :], ic == 0, ic == GDC - 1)
                        ch = g * GDC + oc
                        cx.op("act", lambda e, ch=ch, pb=pb: e.activation(out=ypT[:, ch, :], in_=pb[:, 0:TB], func=AF.Copy, scale=psc[:, ch:ch + 1]),
                              reads=[pb, psc], writes=[ypT])

            for br in range(2):
                for j in range(D // 256):
                    def load(j=j, br=br):
                        wb_, nk = (wbp16, PWC) if br == 0 else (wba16, H)
                        st["wb", br, j] = rslot(nk, 256)
                        b, v = st["wb", br, j]
                        wload(b, v, wb_, kview(wb_, 0, nk, j * 256, 256))
                        st["wg", br, j] = rslot(KC, 256)
                        b, v = st["wg", br, j]
                        wload(b, v, Wi16, kview(Wi16, 0, KC, PW + br * D + j * 256, 256))

                    def comp(j=j, br=br):
                        if br == 0 and j == 0:
                            pool_mix()
                        src, nk = (ypT, PWC) if br == 0 else (atT, H)
                        b, v = st["wb", br, j]; bg, vg = st["wg", br, j]
                        for oc in range(2):
                            ch = 2 * j + oc
                            pa = PB[oc]; pg = PB[2 + oc]; s = sg[oc]
                            for c in range(nk):
                                mm(pa, pa[:, 0:TB], b, v[:, c, oc * 128:(oc + 1) * 128], src, src[:, c, :], c == 0, c == nk - 1)
                            for c in range(KC):
                                mm(pg, pg[:, 0:TB], bg, vg[:, c, oc * 128:(oc + 1) * 128], xTe, xTe[:, c, 16:XW], c == 0, c == KC - 1)
                            cx.op("act", lambda e, s=s, pg=pg: e.activation(out=s[:], in_=pg[:, 0:TB], func=AF.Sigmoid), reads=[pg], writes=[s])
                            if br == 0:
                                cx.op("dve", lambda e, s=s, pa=pa, ch=ch: e.tensor_tensor(out=mg[:, ch, :], in0=pa[:, 0:TB], in1=s[:], op=ALU.mult), reads=[pa, s], writes=[mg])
                            else:
                                cx.op("dve", lambda e, s=s, pa=pa: e.tensor_tensor(out=tmpf[:], in0=pa[:, 0:TB], in1=s[:], op=ALU.mult), reads=[pa, s], writes=[tmpf])
                                cx.op("dve", lambda e, ch=ch: e.tensor_tensor(out=mg[:, ch, :], in0=mg[:, ch, :], in1=tmpf[:], op=ALU.add), reads=[mg, tmpf], writes=[mg])
                    add_unit(load, comp)

            def ln_router():
                for t in range(TT):
                    tok0 = o0 + t * 128
                    tile_i = tok0 // 128
                    ln1(xin4, xin4[:, t, :])
                    cx.op("act", lambda e, t=t: e.copy(out=hb[:], in_=xin4[:, t, :]), reads=[xin4], writes=[hb])
                    dma(sp, H16, H16.t[tok0:tok0 + 128, :], hb, hb[:], sem=hb)
                    transpose_rows(hb, hb[:], 128, KC, hT, lambda c0, n, t=t: hT[:, c0:c0 + n, t * 128:(t + 1) * 128])
                    for g0 in range(0, KC, 4):
                        pb = PB[4 + (g0 // 4) % 2]
                        for i in range(4):
                            c = g0 + i
                            cx.op("pe", lambda e, pb=pb, i=i, c=c, t=t: e.transpose(out=pb[:, i * 128:(i + 1) * 128], in_=xin4[:, t, c * 128:(c + 1) * 128], identity=ident32),
                                  reads=[xin4, cstf], writes=[pb])
                        evac(hT32, hT32[:, g0:g0 + 4, :], pb, pb[:, :].rearrange("p (c n) -> p c n", n=128))
                    pr = PB[0]
                    for c in range(KC):
                        mm(pr, pr[:, 0:E], hT32, hT32[:, c, :], wr, wr[:, c, :], c == 0, c == KC - 1)
                    sc, selb, mask, pos, oh, tmp = (rt[k] for k in ("sc", "selb", "mask", "pos", "oh", "tmp"))
                    cx.op("act", lambda e: e.activation(out=sc[:], in_=pr[:, 0:E], func=AF.Sigmoid), reads=[pr], writes=[sc])
                    cx.op("dve", lambda e: e.tensor_tensor(out=selb[:], in0=sc[:], in1=rbias[:], op=ALU.add), reads=[sc, rbias], writes=[selb])
                    cx.op("dve", lambda e: e.max(out=m8[:], in_=selb[:]), reads=[selb], writes=[m8])
                    cx.op("dve", lambda e: e.tensor_scalar(out=mask[:], in0=selb[:], scalar1=m8[:, TOPK - 1:TOPK], scalar2=None, op0=ALU.is_ge), reads=[selb, m8], writes=[mask])
                    pp = PB[1]
                    mm(pp, pp[:, 0:E], cstf, Us32, mask, mask[:], True, True)
                    mm(pp, pp[:, 128:128 + E], cstf, ones32, mask, mask[:], True, True)
                    cx.op("dve", lambda e: e.tensor_tensor(out=pos[:], in0=pp[:, 0:E], in1=rcar[:], op=ALU.add), reads=[pp, rcar], writes=[pos])
                    cx.op("dve", lambda e: e.tensor_tensor(out=rcar[:], in0=pp[:, 128:128 + E], in1=rcar[:], op=ALU.add), reads=[pp, rcar], writes=[rcar])
                    for k in range(TOPK):
                        cx.op("dve", lambda e, k=k: e.tensor_scalar(out=oh[:], in0=selb[:], scalar1=m8[:, k:k + 1], scalar2=None, op0=ALU.is_equal), reads=[selb, m8], writes=[oh])
                        for qi, srcb in enumerate((sc, pos, None)):
                            sap = iotaE if srcb is None else srcb[:]
                            cx.op("dve", lambda e, sap=sap: e.tensor_tensor(out=tmp[:], in0=oh[:], in1=sap, op=ALU.mult), reads=[oh, cstf] + ([srcb] if srcb else []), writes=[tmp])
                            cx.op("dve", lambda e, qi=qi, k=k: e.reduce_sum(out=r3[:, qi, k:k + 1], in_=tmp[:], axis=AX.X), reads=[tmp], writes=[r3])
                    cx.op("dve", lambda e: e.reduce_sum(out=rsum[:, 0:1], in_=r3[:, 0, 0:TOPK], axis=AX.X), reads=[r3], writes=[rsum])
                    cx.op("dve", lambda e: e.reciprocal(out=rsum[:, 1:2], in_=rsum[:, 0:1]), reads=[rsum], writes=[rsum])
                    cx.op("dve", lambda e, tile_i=tile_i: e.tensor_scalar(out=gates[:, tile_i, 0:TOPK], in0=r3[:, 0, 0:TOPK], scalar1=rsum[:, 1:2], scalar2=None, op0=ALU.mult),
                          reads=[r3, rsum], writes=[gates])
                    cx.op("dve", lambda e: e.tensor_scalar(out=r3[:, 1, 0:TOPK], in0=r3[:, 1, 0:TOPK], scalar1=float(C - 1), scalar2=None, op0=ALU.min), reads=[r3], writes=[r3])
                    cx.op("dve", lambda e: e.scalar_tensor_tensor(out=slf[:, 0:TOPK], in0=r3[:, 2, 0:TOPK], scalar=float(C), in1=r3[:, 1, 0:TOPK], op0=ALU.mult, op1=ALU.add),
                          reads=[r3], writes=[slf])
                    cx.op("dve", lambda e, tile_i=tile_i: e.tensor_copy(out=slots[:, tile_i, 0:TOPK], in_=slf[:, 0:TOPK]), reads=[slf], writes=[slots])

            for j in range(D // 256):
                def load(j=j):
                    st["wo", j] = rslot(KC, 256)
                    b, v = st["wo", j]
                    wload(b, v, wo16, kview(wo16, 0, KC, j * 256, 256))

                def comp(j=j):
                    b, v = st["wo", j]
                    for t in range(TT):
                        pb = PB[t % 2]
                        for c in range(KC):
                            mm(pb, pb[:, 0:256], mg, mg[:, c, t * 128:(t + 1) * 128], b, v[:, c, :], c == 0, c == KC - 1)
                        cx.op("dve", lambda e, t=t, pb=pb, j=j: e.scalar_tensor_tensor(
                            out=xin4[:, t, j * 256:(j + 1) * 256], in0=xin4[:, t, j * 256:(j + 1) * 256], scalar=ALPHA, in1=pb[:, 0:256], op0=ALU.mult, op1=ALU.add),
                            reads=[xin4, pb], writes=[xin4])
                    if j == D // 256 - 1:
                        ln_router()
                add_unit(load, comp)

            for j in range(FS // 256):
                def load(j=j):
                    st["s1", j] = rslot(KC, 256); b, v = st["s1", j]
                    wload(b, v, ws1b, kview(ws1b, 0, KC, j * 256, 256))
                    st["s3", j] = rslot(KC, 256); b, v = st["s3", j]
                    wload(b, v, ws3b, kview(ws3b, 0, KC, j * 256, 256))

                def comp(j=j):
                    b1, v1 = st["s1", j]; b3, v3 = st["s3", j]
                    for oc in range(2):
                        p1 = PB[oc]; p3 = PB[2 + oc]; s = sg[oc]
                        for c in range(KC):
                            mm(p1, p1[:, 0:TB], b1, v1[:, c, oc * 128:(oc + 1) * 128], hT, hT[:, c, :], c == 0, c == KC - 1)
                        for c in range(KC):
                            mm(p3, p3[:, 0:TB], b3, v3[:, c, oc * 128:(oc + 1) * 128], hT, hT[:, c, :], c == 0, c == KC - 1)
                        cx.op("act", lambda e, s=s, p1=p1: e.activation(out=s[:], in_=p1[:, 0:TB], func=AF.Silu), reads=[p1], writes=[s])
                        cx.op("dve", lambda e, s=s, p3=p3, ch=2 * j + oc: e.tensor_tensor(out=aT[:, ch, :], in0=p3[:, 0:TB], in1=s[:], op=ALU.mult), reads=[p3, s], writes=[aT])
                add_unit(load, comp)

            for j in range(D // 256):
                def load(j=j):
                    st["s2", j] = rslot(FSC, 256); b, v = st["s2", j]
                    wload(b, v, ws2b, kview(ws2b, 0, FSC, j * 256, 256))

                def comp(j=j):
                    b, v = st["s2", j]
                    for t in range(TT):
                        pb = PB[4 + t % 2]
                        for c in range(FSC):
                            mm(pb, pb[:, 0:256], aT, aT[:, c, t * 128:(t + 1) * 128], b, v[:, c, :], c == 0, c == FSC - 1)
                        cx.op("dve", lambda e, t=t, pb=pb, j=j: e.scalar_tensor_tensor(
                            out=xin4[:, t, j * 256:(j + 1) * 256], in0=xin4[:, t, j * 256:(j + 1) * 256], scalar=ALPHA, in1=pb[:, 0:256], op0=ALU.mult, op1=ALU.add),
                            reads=[xin4, pb], writes=[xin4])
                    if j == D // 256 - 1:
                        for t in range(TT):
                            tok0 = o0 + t * 128
                            dma(sp, Base, Base.t[tok0:tok0 + 128, :], xin4, xin4[:, t, :], sem=xin4)
                            for k in range(TOPK):
                                ti_ = tok0 // 128
                                cx.dma(pool, lambda e, ti_=ti_, k=k: e.indirect_dma_start(
                                    out=Tab.t[:, :], out_offset=bass.IndirectOffsetOnAxis(ap=slots[:, ti_, k:k + 1], axis=0),
                                    in_=tokid[:, ti_:ti_ + 1], in_offset=None), reads=[slots, tokid, Tab], writes=[], sem=slots)
                add_unit(load, comp)

        def run_units(depth):
            n = len(units)
            for i in range(n + depth):
                if i < n:
                    units[i][0]()
                if i >= depth:
                    units[i - depth][1]()

        for bk in range(NB):
            block_units(bk)
        run_units(1)

        cx.phase()
        units.clear()
        CT = C // 128
        RS5 = max(KC, FC) * 256
        ring5 = [cx.alloc("r5_%d" % i, [128, RS5], BF16) for i in range(6)]
        r5 = {"n": 0}

        def rslot5(nch, ncols):
            b = ring5[r5["n"] % len(ring5)]; r5["n"] += 1
            return b, b[:, 0:nch * ncols].rearrange("p (c n) -> p c n", n=ncols)

        tokt = [cx.alloc("tokt%d" % i, [128, CT], I32) for i in range(2)]
        xg = [cx.alloc("xg%d" % i, [128, D], BF16) for i in range(4)]
        XeT = [cx.alloc("XeT%d" % i, [128, KC, C], BF16) for i in range(2)]
        aTe = cx.alloc("aTe", [128, FC, C], BF16)
        st5 = [cx.alloc("st5_%d" % i, [128, C], F32) for i in range(2)]
        yst = [cx.alloc("yst%d" % i, [128, CT, 256], F32) for i in range(3)]
        gcnt = {"n": 0, "y": 0}

        def expert_units(ex):
            st = {}
            xe = XeT[ex % 2]

            def gather():
                tk = tokt[ex % 2]
                for i in range(CT):
                    dma(sp, tk, tk[:, i:i + 1], Tab, Tab.t[ex * C + i * 128:ex * C + (i + 1) * 128, :])
                for i in range(CT):
                    g = xg[gcnt["n"] % 4]; gcnt["n"] += 1
                    cx.dma(pool, lambda e, g=g, tk=tk, i=i: e.indirect_dma_start(
                        out=g[:], out_offset=None, in_=H16.t[:, :], in_offset=bass.IndirectOffsetOnAxis(ap=tk[:, i:i + 1], axis=0)),
                        reads=[tk, H16], writes=[g])
                    transpose_rows(g, g[:], 128, KC, xe, lambda c0, n, i=i: xe[:, c0:c0 + n, i * 128:(i + 1) * 128])

            n13 = (F + 255) // 256
            for j in range(n13):
                ncols = min(256, F - j * 256)

                def load(j=j, ncols=ncols):
                    if j == 0:
                        gather()
                    st["w1", j] = rslot5(KC, ncols); b, v = st["w1", j]
                    wload(b, v, w1, w1.t[ex, :, j * 256:j * 256 + ncols].rearrange("(c p) n -> p c n", p=128))
                    st["w3", j] = rslot5(KC, ncols); b, v = st["w3", j]
                    wload(b, v, w3, w3.t[ex, :, j * 256:j * 256 + ncols].rearrange("(c p) n -> p c n", p=128))

                def comp(j=j, ncols=ncols):
                    b1, v1 = st["w1", j]; b3, v3 = st["w3", j]
                    for oc in range(ncols // 128):
                        p1 = PB[oc]; p3 = PB[2 + oc]; s = st5[oc]
                        for c in range(KC):
                            mm(p1, p1[:, 0:C], b1, v1[:, c, oc * 128:(oc + 1) * 128], xe, xe[:, c, :], c == 0, c == KC - 1)
                        for c in range(KC):
                            mm(p3, p3[:, 0:C], b3, v3[:, c, oc * 128:(oc + 1) * 128], xe, xe[:, c, :], c == 0, c == KC - 1)
                        cx.op("act", lambda e, s=s, p1=p1: e.activation(out=s[:], in_=p1[:, 0:C], func=AF.Silu), reads=[p1], writes=[s])
                        cx.op("dve", lambda e, s=s, p3=p3, ch=2 * j + oc: e.tensor_tensor(out=aTe[:, ch, :], in0=p3[:, 0:C], in1=s[:], op=ALU.mult), reads=[p3, s], writes=[aTe])
                add_unit(load, comp)

            for j in range(D // 256):
                def load(j=j):
                    st["w2", j] = rslot5(FC, 256); b, v = st["w2", j]
                    wload(b, v, w2, w2.t[ex, :, j * 256:(j + 1) * 256].rearrange("(c p) n -> p c n", p=128))

                def comp(j=j):
                    b, v = st["w2", j]
                    ys = yst[gcnt["y"] % 3]; gcnt["y"] += 1
                    for i in range(CT):
                        pb = PB[4 + i % 2]
                        for c in range(FC):
                            mm(pb, pb[:, 0:256], aTe, aTe[:, c, i * 128:(i + 1) * 128], b, v[:, c, :], c == 0, c == FC - 1)
                        evac(ys, ys[:, i, :], pb, pb[:, 0:256])
                    dma(sp, Yb, Yb.t[ex * C:(ex + 1) * C, j * 256:(j + 1) * 256].rearrange("(i p) d -> p i d", p=128), ys, ys[:], sem=ys)
                add_unit(load, comp)

        for ex in range(E):
            expert_units(ex)
        run_units(2)

        cx.phase()
        ln2g = cx.alloc("ln2g", [128, D], F32); ln2b = cx.alloc("ln2b", [128, D], F32)
        dma(sp, ln2g, ln2g[:], ln2g_d, ln2g_d[:, :]); dma(sp, ln2b, ln2b[:], ln2b_d, ln2b_d[:, :])
        ln2 = layernorm(None, None, ln2g, ln2b, "2")
        acc = [cx.alloc("acc%d" % i, [128, D], F32) for i in range(2)]
        yk = [cx.alloc("yk%d" % i, [128, D], F32) for i in range(8)]
        ykc = 0
        for t in range(NT):
            a = acc[t % 2]
            dma(sp, a, a[:], Base, Base.t[t * 128:(t + 1) * 128, :])
            for k in range(TOPK):
                y = yk[ykc % 8]; ykc += 1
                cx.dma(pool, lambda e, y=y, t=t, k=k: e.indirect_dma_start(
                    out=y[:], out_offset=None, in_=Yb.t[:, :], in_offset=bass.IndirectOffsetOnAxis(ap=slots[:, t, k:k + 1], axis=0)),
                    reads=[slots, Yb], writes=[y])
                cx.op("dve", lambda e, y=y, a=a, t=t, k=k: e.scalar_tensor_tensor(
                    out=a[:], in0=y[:], scalar=gates[:, t, k:k + 1], in1=a[:], op0=ALU.mult, op1=ALU.add), reads=[y, a, gates], writes=[a])
            ln2(a, a[:])
            dma(sp, out_d, out_d.t[(sh - cfg.SH0) * TO + t * 128:(sh - cfg.SH0) * TO + (t + 1) * 128, :], a, a[:], sem=a)

    for sh in range(cfg.SH0, cfg.NSH):
        shard_phases(sh)

    cx.emit()
    return nc


def make_consts(cfg):
    E = cfg.E
    k = np.arange(128)[:, None]; m = np.arange(128)[None, :]
    cst = np.zeros((128, 5 * 128 + E), np.float32)
    cst[:, 0:128] = np.eye(128)
    cst[:, 128:256] = (k <= m)
    cst[:, 256:384] = (k < m)
    cst[:, 384:512] = 1.0
    cst[0, 512:640] = 1.0
    cst[:, 640:640 + E] = np.arange(E)[None, :]
    tokid = (np.arange(cfg.NT)[None, :] * 128 + np.arange(128)[:, None]).astype(np.int32)
    return cst, tokid


def prep_inputs(cfg, inp):
    D, TO, NV, H, E = cfg.D, cfg.TO, cfg.NV, cfg.H, cfg.E
    x = np.asarray(inp["x"], np.float32)
    cst, tokid = make_consts(cfg)

    def rep(v, n):
        return np.ascontiguousarray(np.broadcast_to(np.asarray(v, np.float32).reshape(1, n), (128, n)))
    shared = {
        "cst": cst, "tokid": tokid,
        "w_in": np.ascontiguousarray(inp["w_in"][0]), "bfor": rep(inp["b_forget"][0], H),
        "pool_w": np.ascontiguousarray(inp["pool_w"][0]),
        "pscale": np.ascontiguousarray(np.asarray(inp["pool_scale"][0], np.float32).reshape(cfg.PWC, 128).T),
        "wbp": np.ascontiguousarray(inp["w_branch_pool"][0]), "wba": np.ascontiguousarray(inp["w_branch_attn"][0]),
        "w_out": np.ascontiguousarray(inp["w_out"][0]), "ln1g": rep(inp["ln1_g"][0], D), "ln1b": rep(inp["ln1_b"][0], D),
        "w_router": np.ascontiguousarray(inp["w_router"][0]), "rbias": rep(inp["router_bias"][0], E),
        "w1": np.ascontiguousarray(inp["w1"][0]), "w3": np.ascontiguousarray(inp["w3"][0]), "w2": np.ascontiguousarray(inp["w2"][0]),
        "ws1": np.ascontiguousarray(inp["w_shared1"][0]), "ws3": np.ascontiguousarray(inp["w_shared3"][0]),
        "ws2": np.ascontiguousarray(inp["w_shared2"][0]), "ln2g": rep(inp["ln2_g"][0], D), "ln2b": rep(inp["ln2_b"][0], D),
    }
    maps = []
    NOWN = (cfg.NSH - cfg.SH0) * TO
    per_b = cfg.NCORES // 2
    for c in range(cfg.NCORES):
        b, j = c // per_b, c % per_b
        end = (j + 1) * NOWN
        xvv = np.zeros((NV, D), np.float32)
        kb = np.full((NV,), NEG, np.float32)
        n_real = min(end, NV)
        xvv[NV - n_real:] = x[b, end - n_real:end]
        kb[NV - n_real:] = 0.0
        kbias = np.ascontiguousarray(kb.reshape(cfg.NKT, 128).T)
        corr = np.zeros((128, 64), np.float32)
        for g, w in enumerate(WINDOWS):
            pos = j * NOWN + np.arange(16)
            corr[:, g * 16:(g + 1) * 16] = (1.0 / np.minimum(pos + 1, w))[None, :]
        m = dict(shared)
        m.update({"xv": xvv, "kbias": kbias, "corr": corr})
        maps.append(m)
    return maps


_NC_CACHE = {}


def run(cfg, inp):
    key = (cfg.D, cfg.TO, cfg.E, cfg.TOPK, cfg.F, cfg.FS, cfg.C, cfg.TB, cfg.NSH, cfg.SH0)
    if key not in _NC_CACHE:
        _NC_CACHE[key] = build(cfg)
    nc = _NC_CACHE[key]
    maps = prep_inputs(cfg, inp)
    res = run_bass_kernel_spmd(nc, maps, core_ids=list(range(len(maps))))
    outs = [np.asarray(r["out"], np.float32) for r in res.results]
    per_b = cfg.NCORES // 2
    return np.stack([np.concatenate(outs[b * per_b:(b + 1) * per_b], axis=0) for b in range(2)], axis=0)


def kernel(**inputs):
    return run(Cfg(), inputs)
```

```python
from contextlib import ExitStack
import numpy as np
import concourse.bass as bass
import concourse.mybir as mybir
from concourse.bass_utils import run_bass_kernel_spmd

F32 = mybir.dt.float32
BF16 = mybir.dt.bfloat16
I32 = mybir.dt.int32
AF = mybir.ActivationFunctionType
ALU = mybir.AluOpType
AX = mybir.AxisListType
WINDOWS = (2, 4, 8, 16)
LN_EPS = 1e-5
ALPHA = 2.0 ** 0.25
NEG = -30000.0


class Cfg:
    def __init__(self, D=2048, TO=4096, E=64, TOPK=6, F=1408, FS=2816, C=512, TB=256, NSH=4, SH0=2):
        self.D, self.TO, self.E, self.TOPK, self.F, self.FS, self.C, self.TB = D, TO, E, TOPK, F, FS, C, TB
        self.NSH = NSH
        self.SH0 = SH0
        self.NCORES = 2 * NSH // (NSH - SH0)
        self.NV = self.NSH * TO
        self.KC = D // 128
        self.PW = D // 2
        self.PWC = self.PW // 128
        self.GD = self.PW // 4
        self.GDC = self.GD // 128
        self.H = (D // 2) // 128
        self.AW = self.H * 128
        self.OFF_Q = self.PW
        self.OFF_K = self.OFF_Q + self.AW
        self.OFF_V = self.OFF_K + self.AW
        self.OFF_F = self.OFF_V + self.AW
        self.OFF_G = self.OFF_F + self.H
        self.INW = self.OFF_G + 2 * D
        self.FC = F // 128
        self.FSC = FS // 128
        self.NKT = self.NV // 128
        self.NT = TO // 128
        self.NQS = TO // 256


class Buf:
    __slots__ = ("name", "t", "last_w", "readers", "dsem", "track")

    def __init__(self, name, t=None, track=True):
        self.name = name
        self.t = t
        self.last_w = None
        self.readers = {}
        self.dsem = None
        self.track = track

    def __getitem__(self, k):
        return self.t[k]


class Ctx:
    ENG = ("pe", "act", "dve", "pool", "sp")
    ARENA = 103000

    def __init__(self, nc):
        self.nc = nc
        self.es = ExitStack()
        self.q = {e: [] for e in self.ENG}
        self.cnt = {e: 0 for e in self.ENG}
        self.dcnt = {}
        self.ndsem = 0
        self.arena = self.es.enter_context(nc.sbuf_tensor("arena", [128, self.ARENA], BF16))
        self.ptr = 0
        self.mark = 0

    def alloc(self, name, shape, dt):
        n = int(np.prod(shape[1:]))
        nb = n * (2 if dt != BF16 else 1)
        self.ptr = (self.ptr + 15) // 16 * 16
        assert self.ptr + nb <= self.ARENA, (name, self.ptr, nb)
        v = self.arena[0:shape[0], self.ptr:self.ptr + nb]
        self.ptr += nb
        if dt != BF16:
            v = v.bitcast(dt)
        if len(shape) == 3:
            v = v.rearrange("p (a b) -> p a b", a=shape[1])
        elif len(shape) == 4:
            v = v.rearrange("p (a b c) -> p a b c", a=shape[1], b=shape[2])
        return Buf(name, v)

    def ps(self, name, shape, dt=F32):
        t = self.es.enter_context(self.nc.psum_tensor(name, list(shape), dt))
        return Buf(name, t)

    def phase(self):
        self.barrier()
        self.ptr = self.mark
        self.ndsem = self.ndsem_mark

    def keep(self):
        self.mark = self.ptr
        self.ndsem_mark = self.ndsem

    def barrier(self):
        allw = [(("c", e), v) for e, v in self.cnt.items() if v] + [(("d", n), v) for n, v in self.dcnt.items()]
        for e in self.ENG:
            self.q[e].append((list(allw), None, None, 0))

    def _deps(self, eng, reads, writes):
        w = {}

        def add(tok):
            if tok is None:
                return
            k, v = tok
            if k[0] == "c" and k[1] == eng and eng == "pe":
                return
            if k[0] == "d":
                v = self.dcnt[k[1]]
            if w.get(k, 0) < v:
                w[k] = v
        for r in reads:
            add(r.last_w)
        for b in writes:
            add(b.last_w)
            for k, v in b.readers.items():
                add((k, v))
        return list(w.items())

    def _commit(self, tok, reads, writes):
        k, v = tok
        for r in reads:
            if r.readers.get(k, 0) < v:
                r.readers[k] = v
        for b in writes:
            b.last_w = tok
            b.readers = {}

    def op(self, eng, fn, reads=(), writes=()):
        waits = self._deps(eng, reads, writes)
        self.cnt[eng] += 1
        tok = (("c", eng), self.cnt[eng])
        self.q[eng].append((waits, fn, tok[0], 1))
        self._commit(tok, reads, writes)

    def dma(self, eng, fn, reads=(), writes=(), sem=None):
        reads = [b for b in reads if b.track]
        writes = [b for b in writes if b.track]
        if sem is None:
            sem = (list(writes) + list(reads))[0]
        if sem.dsem is None:
            sem.dsem = self.ndsem
            self.ndsem += 1
        name = sem.dsem
        waits = self._deps(eng, reads, writes)
        self.dcnt[name] = self.dcnt.get(name, 0) + 16
        tok = (("d", name), self.dcnt[name])
        self.q[eng].append((waits, fn, tok[0], 16))
        self._commit(tok, reads, writes)

    def emit(self):
        nc = self.nc
        self.barrier()
        sems = {}
        for e in self.ENG:
            if self.cnt[e]:
                sems[("c", e)] = self.es.enter_context(nc.semaphore("c_" + e))
        for name in self.dcnt:
            sems[("d", name)] = self.es.enter_context(nc.semaphore("d_%d" % name))
        block = self.es.enter_context(nc.Block())

        def run(e, engobj):
            waited = {}
            for waits, fn, sk, amt in self.q[e]:
                for k, v in waits:
                    if waited.get(k, 0) < v:
                        engobj.wait_ge(sems[k], v)
                        waited[k] = v
                if fn is not None:
                    fn(engobj).then_inc(sems[sk], amt)

        block.tensor(lambda x: run("pe", x))
        block.scalar(lambda x: run("act", x))
        block.vector(lambda x: run("dve", x))
        block.gpsimd(lambda x: run("pool", x))
        block.sync(lambda x: run("sp", x))
        self.es.close()


def build(cfg):
    nc = bass.Bass("TRN2", target_bir_lowering=False)
    D, TO, E, TOPK, F, FS, C, TB = cfg.D, cfg.TO, cfg.E, cfg.TOPK, cfg.F, cfg.FS, cfg.C, cfg.TB
    NV, KC, PW, PWC, GD, GDC, H, AW = cfg.NV, cfg.KC, cfg.PW, cfg.PWC, cfg.GD, cfg.GDC, cfg.H, cfg.AW
    FC, FSC, NKT, NT, NQS = cfg.FC, cfg.FSC, cfg.NKT, cfg.NT, cfg.NQS
    NCST = 5 * 128 + E

    def ein(name, shape, dt=F32):
        return Buf(name, nc.dram_tensor(name, list(shape), dt, kind="ExternalInput").ap(), track=False)

    def scr(name, shape, dt):
        return Buf(name, nc.dram_tensor(name, list(shape), dt, kind="Internal").ap(), track=False)

    xv = ein("xv", [NV, D]); kbias_d = ein("kbias", [128, NKT]); corr_d = ein("corr", [128, 64])
    cst_d = ein("cst", [128, NCST]); tokid_d = ein("tokid", [128, NT], I32)
    w_in = ein("w_in", [D, cfg.INW]); bfor_d = ein("bfor", [128, H]); pool_w = ein("pool_w", [4, GD, GD])
    pscale_d = ein("pscale", [128, PWC]); wbp = ein("wbp", [PW, D]); wba = ein("wba", [AW, D])
    w_out = ein("w_out", [D, D]); ln1g_d = ein("ln1g", [128, D]); ln1b_d = ein("ln1b", [128, D])
    w_router = ein("w_router", [D, E]); rbias_d = ein("rbias", [128, E])
    w1 = ein("w1", [E, D, F]); w3 = ein("w3", [E, D, F]); w2 = ein("w2", [E, F, D])
    ws1 = ein("ws1", [D, FS]); ws3 = ein("ws3", [D, FS]); ws2 = ein("ws2", [FS, D])
    ln2g_d = ein("ln2g", [128, D]); ln2b_d = ein("ln2b", [128, D])
    out_d = Buf("out", nc.dram_tensor("out", [(cfg.NSH - cfg.SH0) * TO, D], F32, kind="ExternalOutput").ap(), track=False)

    Ks = scr("Ks", [H, 128, NV], BF16); Vs = scr("Vs", [H, NV, 128], BF16)
    Qs = scr("Qs", [H, 128, NV], BF16); As = scr("As", [H, 128, NV], BF16)
    Gs = scr("Gs", [128, NKT * H], F32)
    H32 = scr("H32", [TO, D], F32); H16 = scr("H16", [TO, D], BF16); Base = scr("Base", [TO, D], F32)
    Tab = scr("Tab", [E * C, 1], I32); Tab.track = True; Yb = scr("Yb", [E * C, D], F32)

    cx = Ctx(nc)
    sp, pool = "sp", "pool"

    def dma(eng, out_b, out_ap, in_b, in_ap, sem=None, **kw):
        cx.dma(eng, lambda e: e.dma_start(out=out_ap, in_=in_ap, **kw), reads=[in_b], writes=[out_b], sem=sem)

    def mm(ob, oap, lb, lap, rb, rap, start, stop):
        cx.op("pe", lambda e: e.matmul(oap, lhsT=lap, rhs=rap, start=start, stop=stop), reads=[lb, rb], writes=[ob])

    rr = {"n": 0}

    def evac(ob, oap, ib, iap):
        rr["n"] += 1
        if rr["n"] % 2:
            cx.op("act", lambda e: e.copy(out=oap, in_=iap), reads=[ib], writes=[ob])
        else:
            cx.op("dve", lambda e: e.tensor_copy(out=oap, in_=iap), reads=[ib], writes=[ob])

    PB = [cx.ps("pb%d" % i, [128, 512], F32) for i in range(6)]
    TBK = [cx.ps("tb%d" % i, [128, 1024], BF16) for i in range(2)]
    tcnt = {"n": 0}

    cstf = cx.alloc("cstf", [128, NCST], F32)
    ident32 = cstf[:, 0:128]; U32 = cstf[:, 128:256]; Us32 = cstf[:, 256:384]; ones32 = cstf[:, 384:512]
    E032 = cstf[:, 512:640]; iotaE = cstf[:, 640:640 + E]
    cstb = cx.alloc("cstb", [128, 384], BF16)
    identb = cstb[:, 0:128]; trib = cstb[:, 128:256]; onesb = cstb[:, 256:384]
    tokid = cx.alloc("tokid", [128, NT], I32)
    gates = cx.alloc("gates", [128, NT, 8], F32)
    slots = cx.alloc("slots", [128, NT, 8], I32)
    dma(sp, cstf, cstf[:], cst_d, cst_d[:, :])
    dma(sp, tokid, tokid[:], tokid_d, tokid_d[:, :])
    cx.op("dve", lambda e: e.tensor_copy(out=cstb[:, 0:256], in_=cstf[:, 0:256]), reads=[cstf], writes=[cstb])
    cx.op("dve", lambda e: e.tensor_copy(out=cstb[:, 256:384], in_=cstf[:, 384:512]), reads=[cstf], writes=[cstb])
    cx.keep()

    def transpose_rows(src_b, src_ap, nrows, nchunks, dst_b, dst_fn, ident=None):
        for g0 in range(0, nchunks, 8):
            n = min(8, nchunks - g0)
            tb = TBK[tcnt["n"] % 2]; tcnt["n"] += 1
            for i in range(n):
                c = g0 + i
                cx.op("pe", lambda e, tb=tb, i=i, c=c: e.transpose(
                    out=tb[:, i * 128:i * 128 + nrows], in_=src_ap[:, c * 128:(c + 1) * 128],
                    identity=identb[0:nrows, 0:nrows]), reads=[src_b, cstb], writes=[tb])
            evac(dst_b, dst_fn(g0, n), tb, tb[:, 0:n * 128].rearrange("p (c n) -> p c n", n=128)[:, :, 0:nrows])

    def wload(slot, shape_ap, src_b, src_ap):
        dma(pool, slot, shape_ap, src_b, src_ap)

    def kview(wb, r0, nchunks, c0, ncols):
        return wb.t[r0:r0 + nchunks * 128, c0:c0 + ncols].rearrange("(c p) n -> p c n", p=128)

    def layernorm(xb, xap, gb, bb, tag):
        FM = min(D, int(nc.vector.BN_STATS_FMAX))
        nst = D // FM
        st = cx.alloc("lnst" + tag, [128, nst, int(nc.vector.BN_STATS_DIM)], F32)
        mv = cx.alloc("lnmv" + tag, [128, int(nc.vector.BN_AGGR_DIM)], F32)
        sd = cx.alloc("lnsd" + tag, [128, 2], F32)

        def run(xb, xap):
            for c in range(nst):
                cx.op("dve", lambda e, c=c: e.bn_stats(out=st[:, c, :], in_=xap[:, c * FM:(c + 1) * FM]), reads=[xb], writes=[st])
            cx.op("dve", lambda e: e.bn_aggr(out=mv[:], in_=st[:]), reads=[st], writes=[mv])
            cx.op("act", lambda e: e.activation(out=sd[:, 0:1], in_=mv[:, 1:2], func=AF.Sqrt, bias=LN_EPS, scale=1.0), reads=[mv], writes=[sd])
            cx.op("dve", lambda e: e.reciprocal(out=sd[:, 1:2], in_=sd[:, 0:1]), reads=[sd], writes=[sd])
            cx.op("dve", lambda e: e.tensor_scalar(out=xap, in0=xap, scalar1=mv[:, 0:1], scalar2=sd[:, 1:2],
                                                  op0=ALU.subtract, op1=ALU.mult), reads=[xb, mv, sd], writes=[xb])
            cx.op("dve", lambda e: e.tensor_tensor(out=xap, in0=xap, in1=gb[:], op=ALU.mult), reads=[xb, gb], writes=[xb])
            cx.op("dve", lambda e: e.tensor_tensor(out=xap, in0=xap, in1=bb[:], op=ALU.add), reads=[xb, bb], writes=[xb])
        return run

    def xpass(which):
        cx.phase()
        XB = 512
        if which == "kvf":
            ncol = 2 * AW + H
            wres = cx.alloc("wres", [128, KC, ncol], BF16)
            for c in range(KC):
                wload(wres, wres[:, c, 0:2 * AW], w_in, w_in.t[c * 128:(c + 1) * 128, cfg.OFF_K:cfg.OFF_K + 2 * AW])
                wload(wres, wres[:, c, 2 * AW:ncol], w_in, w_in.t[c * 128:(c + 1) * 128, cfg.OFF_F:cfg.OFF_F + H])
            blocks = range(NV // XB)
            row_base = 0
        else:
            wres = cx.alloc("wres", [128, KC, AW], BF16)
            for c in range(KC):
                wload(wres, wres[:, c, :], w_in, w_in.t[c * 128:(c + 1) * 128, cfg.OFF_Q:cfg.OFF_Q + AW])
            blocks = range(cfg.SH0 * TO // XB, NV // XB)
            row_base = 0
        xin = [cx.alloc("xin%d" % i, [128, D], F32) for i in range(3)]
        xbf = [cx.alloc("xbf%d" % i, [128, D], BF16) for i in range(2)]
        xT = [cx.alloc("xT%d" % i, [128, KC, XB], BF16) for i in range(2)]
        kst = [cx.alloc("kst%d" % i, [128, H, XB], BF16) for i in range(2)]
        if which == "kvf":
            vst = [cx.alloc("vst%d" % i, [128, XB // 128, AW], BF16) for i in range(2)]
            bfor = cx.alloc("bfor", [128, H], F32)
            dma(sp, bfor, bfor[:], bfor_d, bfor_d[:, :])
            Gtok = cx.alloc("Gtok", [128, NKT, H], F32)
            carry = cx.alloc("carry", [128, H], F32)
            cx.op("dve", lambda e: e.memset(carry[:], 0.0), writes=[carry])
            fb = [cx.alloc("fb%d" % i, [128, H], F32) for i in range(2)]
        ti = 0
        for bi, blk in enumerate(blocks):
            r0 = row_base + blk * XB
            xt = xT[bi % 2]
            for t in range(XB // 128):
                xi = xin[ti % 3]; xb = xbf[ti % 2]; ti += 1
                dma(sp, xi, xi[:], xv, xv.t[r0 + t * 128:r0 + (t + 1) * 128, :])
                if ti % 2:
                    cx.op("act", lambda e, xi=xi, xb=xb: e.copy(out=xb[:], in_=xi[:]), reads=[xi], writes=[xb])
                else:
                    cx.op("dve", lambda e, xi=xi, xb=xb: e.tensor_copy(out=xb[:], in_=xi[:]), reads=[xi], writes=[xb])
                transpose_rows(xb, xb[:], 128, KC, xt, lambda c0, n, xt=xt, t=t: xt[:, c0:c0 + n, t * 128:(t + 1) * 128])
            ks = kst[bi % 2]
            for h in range(H):
                pb = PB[h % 2]
                for c in range(KC):
                    mm(pb, pb[:, :], wres, wres[:, c, h * 128:(h + 1) * 128], xt, xt[:, c, :], c == 0, c == KC - 1)
                if which == "q":
                    rr["n"] += 1
                    cx.op("act", lambda e, ks=ks, h=h, pb=pb: e.mul(out=ks[:, h, :], in_=pb[:, :], mul=128.0 ** -0.5), reads=[pb], writes=[ks])
                else:
                    evac(ks, ks[:, h, :], pb, pb[:, :])
            if which == "q":
                dma(sp, Qs, Qs.t[:, :, blk * XB:(blk + 1) * XB].rearrange("h d t -> d h t"), ks, ks[:], sem=ks)
                continue
            dma(sp, Ks, Ks.t[:, :, r0:r0 + XB].rearrange("h d t -> d h t"), ks, ks[:], sem=ks)
            vs = vst[bi % 2]
            for t in range(XB // 128):
                for nb in range(0, AW, 512):
                    pb = PB[2 + (t + nb // 512) % 2]
                    for c in range(KC):
                        mm(pb, pb[:, :], xt, xt[:, c, t * 128:(t + 1) * 128], wres, wres[:, c, AW + nb:AW + nb + 512], c == 0, c == KC - 1)
                    evac(vs, vs[:, t, nb:nb + 512], pb, pb[:, :])
            for t in range(XB // 128):
                dma(sp, Vs, Vs.t[:, r0 + t * 128:r0 + (t + 1) * 128, :].rearrange("h p d -> p h d"), vs,
                    vs[:, t, :].rearrange("p (h d) -> p h d", d=128), sem=vs)
            for t in range(XB // 128):
                kt = (r0 // 128) + t
                pf = PB[4]; pg = PB[5]; f = fb[t % 2]
                for c in range(KC):
                    mm(pf, pf[:, 0:H], xt, xt[:, c, t * 128:(t + 1) * 128], wres, wres[:, c, 2 * AW:2 * AW + H], c == 0, c == KC - 1)
                cx.op("dve", lambda e, f=f, pf=pf: e.tensor_tensor(out=f[:], in0=pf[:, 0:H], in1=bfor[:], op=ALU.add), reads=[pf, bfor], writes=[f])
                cx.op("act", lambda e, f=f: e.activation(out=f[:], in_=f[:], func=AF.Exp, scale=-1.0), reads=[f], writes=[f])
                cx.op("act", lambda e, f=f: e.activation(out=f[:], in_=f[:], func=AF.Ln, bias=1.0, scale=1.0), reads=[f], writes=[f])
                mm(pg, pg[:, 0:H], cstf, U32, f, f[:], True, True)
                mm(pg, pg[:, 64:64 + H], cstf, ones32, f, f[:], True, True)
                cx.op("dve", lambda e, kt=kt, pg=pg: e.tensor_tensor(out=Gtok[:, kt, :], in0=pg[:, 0:H], in1=carry[:], op=ALU.add), reads=[pg, carry], writes=[Gtok])
                cx.op("dve", lambda e, pg=pg: e.tensor_tensor(out=carry[:], in0=pg[:, 64:64 + H], in1=carry[:], op=ALU.add), reads=[pg, carry], writes=[carry])
        if which == "kvf":
            dma(sp, Gs, Gs[:, :], Gtok, Gtok[:].rearrange("p a b -> p (a b)"), sem=Gtok)

    Wi16 = scr("Wi16", [D, PW + 2 * D], BF16); wbp16 = scr("wbp16", [PW, D], BF16); wba16 = scr("wba16", [AW, D], BF16)
    wo16 = scr("wo16", [D, D], BF16); ws1b = scr("ws1b", [D, FS], BF16); ws3b = scr("ws3b", [D, FS], BF16); ws2b = scr("ws2b", [FS, D], BF16)
    cx.phase()
    bnc = [cx.alloc("bnc%d" % i, [128, PW + 2 * D], BF16) for i in range(3)]
    bstate = {"n": 0}

    def precast(dst, src, nrows, segs):
        for r in range(0, nrows, 128):
            b = bnc[bstate["n"] % 3]; bstate["n"] += 1
            tot = 0
            for (sc, n, dc) in segs:
                dma(pool, b, b[:, dc:dc + n], src, src.t[r:r + 128, sc:sc + n])
                tot = max(tot, dc + n)
            dma(sp, dst, dst.t[r:r + 128, 0:tot], b, b[:, 0:tot], sem=b)

    precast(Wi16, w_in, D, [(0, PW, 0), (cfg.OFF_G, 2 * D, PW)])
    precast(wbp16, wbp, PW, [(0, D, 0)]); precast(wba16, wba, AW, [(0, D, 0)]); precast(wo16, w_out, D, [(0, D, 0)])
    precast(ws1b, ws1, D, [(0, FS, 0)]); precast(ws3b, ws3, D, [(0, FS, 0)]); precast(ws2b, ws2, FS, [(0, D, 0)])

    xpass("kvf")
    xpass("q")

    def shard_phases(sh):
        cx.phase()
        Gtok = cx.alloc("Gtok3", [128, NKT, H], F32)
        kb = cx.alloc("kb3", [128, NKT], F32)
        dma(sp, Gtok, Gtok[:].rearrange("p a b -> p (a b)"), Gs, Gs[:, :])
        dma(sp, kb, kb[:], kbias_d, kbias_d[:, :])
        KT = [cx.alloc("KT%d" % i, [128, NV], BF16) for i in range(2)]
        VH = [cx.alloc("VH%d" % i, [128, NKT, 128], BF16) for i in range(2)]
        QT = [cx.alloc("QT%d" % i, [128, TO], BF16) for i in range(2)]
        BT = [cx.alloc("BT%d" % i, [128, NKT, NQS], F32) for i in range(2)]
        gk = cx.alloc("gk", [128, NKT], F32)
        gm = cx.alloc("gm", [128, NQS], F32)
        Pt = [cx.alloc("P%d" % i, [128, 512], BF16) for i in range(4)]
        rec = cx.alloc("rec", [128, 512], F32)
        ost = [cx.alloc("ost%d" % i, [128, 512], BF16) for i in range(2)]
        KT0 = sh * TO // 128
        pcnt = 0
        qbcnt = 0

        def load_head(h):
            dma(sp, KT[h % 2], KT[h % 2][:], Ks, Ks.t[h, :, :])
            for t0 in range(0, NKT, 16):
                n = min(16, NKT - t0)
                dma(sp, VH[h % 2], VH[h % 2][:, t0:t0 + n, :], Vs,
                    Vs.t[h, t0 * 128:(t0 + n) * 128, :].rearrange("(t p) d -> p t d", p=128))
            dma(sp, QT[h % 2], QT[h % 2][:], Qs, Qs.t[h, :, sh * TO:(sh + 1) * TO])

        load_head(0)
        for h in range(H):
            if h + 1 < H:
                load_head(h + 1)
            kt_, vh_, qt_, bt_ = KT[h % 2], VH[h % 2], QT[h % 2], BT[h % 2]
            pgm = PB[4]
            mids = Gtok[:, KT0:KT0 + 2 * NQS, h].rearrange("p (q two) -> p q two", two=2)[:, :, 1]
            cx.op("dve", lambda e, h=h: e.tensor_tensor(out=gk[:], in0=Gtok[:, :, h], in1=kb[:], op=ALU.add), reads=[Gtok, kb], writes=[gk])
            cx.op("dve", lambda e, mids=mids: e.tensor_copy(out=gm[:], in_=mids), reads=[Gtok], writes=[gm])
            mm(pgm, pgm[:, 0:NQS], cstf, E032, gm, gm[:], True, True)
            cx.op("act", lambda e: e.copy(out=gm[:], in_=pgm[:, 0:NQS]), reads=[pgm], writes=[gm])
            cx.op("dve", lambda e, bt_=bt_: e.tensor_tensor(
                out=bt_[:], in0=gk[:].unsqueeze(2).to_broadcast([128, NKT, NQS]),
                in1=gm[:].unsqueeze(1).to_broadcast([128, NKT, NQS]), op=ALU.subtract), reads=[gk, gm], writes=[bt_])
            for qb in range(TO // 512):
                qbcnt += 1
                po = PB[qbcnt % 2]; pl = PB[4 + qbcnt % 2]
                nk = KT0 + 4 * qb + 4
                items = []
                for kt in range(nk):
                    col0 = max(0, kt - (KT0 + 4 * qb)) * 128
                    items.append((kt, col0))

                def emit_s(i):
                    kt, col0 = items[i]
                    pss = PB[2 + i % 2]
                    mm(pss, pss[:, col0:512], kt_, kt_[:, kt * 128:(kt + 1) * 128], qt_, qt_[:, qb * 512 + col0:(qb + 1) * 512], True, True)

                def emit_pv(i):
                    nonlocal pcnt
                    kt, col0 = items[i]
                    pss = PB[2 + i % 2]
                    p = Pt[pcnt % 4]; pcnt += 1
                    for half in range(2):
                        a = max(col0, half * 256); b = (half + 1) * 256
                        if a >= b:
                            continue
                        bap = bt_[:, kt, 2 * qb + half:2 * qb + half + 1]
                        cx.op("act", lambda e, a=a, b=b, p=p, pss=pss, bap=bap: e.activation(
                            out=p[:, a:b], in_=pss[:, a:b], func=AF.Exp, bias=bap, scale=1.0),
                            reads=[pss, bt_], writes=[p])
                    if kt >= KT0 + 4 * qb:
                        cx.op("pool", lambda e, p=p, col0=col0: e.tensor_tensor(out=p[:, col0:col0 + 128], in0=p[:, col0:col0 + 128], in1=trib, op=ALU.mult),
                              reads=[p, cstb], writes=[p])
                    mm(po, po[:, col0:512], vh_, vh_[:, kt, :], p, p[:, col0:512], i == 0, i == len(items) - 1)
                    mm(pl, pl[:, col0:512], cstb, onesb, p, p[:, col0:512], i == 0, i == len(items) - 1)

                for i in range(len(items) + 1):
                    if i < len(items):
                        emit_s(i)
                    if i >= 1:
                        emit_pv(i - 1)
                o = ost[qb % 2]
                cx.op("dve", lambda e, pl=pl: e.reciprocal(out=rec[:], in_=pl[:, :]), reads=[pl], writes=[rec])
                cx.op("dve", lambda e, o=o, po=po: e.tensor_tensor(out=o[:], in0=po[:, :], in1=rec[:], op=ALU.mult), reads=[po, rec], writes=[o])
                dma(sp, As, As.t[h, :, sh * TO + qb * 512:sh * TO + (qb + 1) * 512], o, o[:], sem=o)

        cx.phase()
        NB = TO // TB
        TT = TB // 128
        XW = TB + 16
        RS = max(KC, FSC, PWC, H) * 256
        ring = [cx.alloc("ring%d" % i, [128, RS], BF16) for i in range(4)]
        rcnt = {"n": 0}

        def rslot(nch, ncols):
            b = ring[rcnt["n"] % len(ring)]; rcnt["n"] += 1
            return b, b[:, 0:nch * ncols].rearrange("p (c n) -> p c n", n=ncols)

        xin4 = cx.alloc("xin4", [128, TT, D], F32)
        xinh = cx.alloc("xinh", [128, D], F32)
        xbf4 = cx.alloc("xbf4", [128, D], BF16)
        xTe = cx.alloc("xTe", [128, KC, XW], BF16)
        u = cx.alloc("u", [128, PWC, XW], F32)
        sA = cx.alloc("sA", [128, PWC, XW], F32)
        sB = cx.alloc("sB", [128, PWC, XW], F32)
        pT = cx.alloc("pT", [128, PWC, TB], BF16)
        ypT = cx.alloc("ypT", [128, PWC, TB], BF16)
        atT = cx.alloc("atT", [128, H, TB], BF16)
        mg = cx.alloc("mg", [128, KC, TB], BF16)
        sg = [cx.alloc("sg%d" % i, [128, TB], F32) for i in range(2)]
        tmpf = cx.alloc("tmpf", [128, TB], F32)
        hb = cx.alloc("hb", [128, D], BF16)
        hT = cx.alloc("hT", [128, KC, TB], BF16)
        hT32 = cx.alloc("hT32", [128, KC, 128], F32)
        aT = cx.alloc("aT", [128, FSC, TB], BF16)
        ln1g = cx.alloc("ln1g", [128, D], F32); ln1b = cx.alloc("ln1b", [128, D], F32)
        wr = cx.alloc("wr", [128, KC, E], F32)
        pw = cx.alloc("pw", [128, 4 * GDC, GD], BF16)
        psc = cx.alloc("psc", [128, PWC], F32)
        corr = cx.alloc("corr", [128, 4, 16], F32)
        rbias = cx.alloc("rbias", [128, E], F32)
        rcar = cx.alloc("rcar", [128, E], F32)
        rt = {k: cx.alloc("rt_" + k, [128, E], F32) for k in ("sc", "selb", "mask", "pos", "oh", "tmp")}
        m8 = cx.alloc("m8", [128, 8], F32)
        r3 = cx.alloc("r3", [128, 3, 8], F32)
        rsum = cx.alloc("rsum", [128, 2], F32)
        slf = cx.alloc("slf", [128, 8], F32)
        zt = cx.alloc("zt", [128, E * C // 128], I32)
        ln1 = layernorm(None, None, ln1g, ln1b, "1")
        dma(sp, ln1g, ln1g[:], ln1g_d, ln1g_d[:, :]); dma(sp, ln1b, ln1b[:], ln1b_d, ln1b_d[:, :])
        dma(sp, wr, wr[:], w_router, w_router.t[:, :].rearrange("(c p) e -> p c e", p=128))
        dma(sp, psc, psc[:], pscale_d, pscale_d[:, :])
        dma(sp, corr, corr[:].rearrange("p a b -> p (a b)"), corr_d, corr_d[:, :])
        dma(sp, rbias, rbias[:], rbias_d, rbias_d[:, :])
        for g in range(4):
            wload(pw, pw[:, g * GDC:(g + 1) * GDC, :], pool_w, pool_w.t[g, :, :].rearrange("(c p) n -> p c n", p=128))
        cx.op("dve", lambda e: e.memset(rcar[:], 0.0), writes=[rcar])
        cx.op("dve", lambda e: e.memset(zt[:], 0), writes=[zt])
        dma(sp, Tab, Tab.t[:, :].rearrange("(p i) o -> p (i o)", p=128), zt, zt[:], sem=zt)

        units = []
        wl = {"n": 0}

        def wload16(slot, shape_ap, src_b, src_ap):
            wl["n"] += 1
            dma(sp if wl["n"] % 2 else pool, slot, shape_ap, src_b, src_ap)

        def add_unit(load, comp):
            units.append((load, comp))

        def block_units(bk):
            r0 = sh * TO + bk * TB
            o0 = bk * TB
            st = {}

            def prologue():
                if r0 == 0:
                    cx.op("dve", lambda e: e.memset(xTe[:, :, 0:16], 0.0), writes=[xTe])
                else:
                    dma(sp, xinh, xinh[0:16, :], xv, xv.t[r0 - 16:r0, :])
                    cx.op("dve", lambda e: e.tensor_copy(out=xbf4[0:16, :], in_=xinh[0:16, :]), reads=[xinh], writes=[xbf4])
                    transpose_rows(xbf4, xbf4[0:16, :], 16, KC, xTe, lambda c0, n: xTe[:, c0:c0 + n, 0:16])
                for t in range(TT):
                    dma(sp, xin4, xin4[:, t, :], xv, xv.t[r0 + t * 128:r0 + (t + 1) * 128, :])
                    cx.op("act", lambda e, t=t: e.copy(out=xbf4[:], in_=xin4[:, t, :]), reads=[xin4], writes=[xbf4])
                    transpose_rows(xbf4, xbf4[:], 128, KC, xTe, lambda c0, n, t=t: xTe[:, c0:c0 + n, 16 + t * 128:16 + (t + 1) * 128])
                dma(sp, atT, atT[:], As, As.t[:, :, sh * TO + o0:sh * TO + o0 + TB].rearrange("h d t -> d h t"))

            for j in range(PW // 256):
                def load(j=j):
                    st["wp", j] = rslot(KC, 256)
                    b, v = st["wp", j]
                    wload16(b, v, Wi16, kview(Wi16, 0, KC, j * 256, 256))

                def comp(j=j):
                    if j == 0:
                        prologue()
                    b, v = st["wp", j]
                    for oc in range(2):
                        pb = PB[oc]
                        for c in range(KC):
                            mm(pb, pb[:, 0:XW], b, v[:, c, oc * 128:(oc + 1) * 128], xTe, xTe[:, c, :], c == 0, c == KC - 1)
                        evac(u, u[:, 2 * j + oc, :], pb, pb[:, 0:XW])
                add_unit(load, comp)

            def pool_mix():
                cpg = GDC
                cx.op("dve", lambda e: e.tensor_tensor(out=sA[:, :, 1:XW], in0=u[:, :, 1:XW], in1=u[:, :, 0:XW - 1], op=ALU.add), reads=[u], writes=[sA])
                cx.op("dve", lambda e: e.tensor_tensor(out=sB[:, cpg:, 3:XW], in0=sA[:, cpg:, 3:XW], in1=sA[:, cpg:, 1:XW - 2], op=ALU.add), reads=[sA], writes=[sB])
                cx.op("dve", lambda e: e.tensor_tensor(out=sA[:, 2 * cpg:, 7:XW], in0=sB[:, 2 * cpg:, 7:XW], in1=sB[:, 2 * cpg:, 3:XW - 4], op=ALU.add), reads=[sB], writes=[sA])
                cx.op("dve", lambda e: e.tensor_tensor(out=sB[:, 3 * cpg:, 15:XW], in0=sA[:, 3 * cpg:, 15:XW], in1=sA[:, 3 * cpg:, 7:XW - 8], op=ALU.add), reads=[sA], writes=[sB])
                srcs = [sA, sB, sA, sB]
                for g in range(4):
                    s = srcs[g]; cs = slice(g * cpg, (g + 1) * cpg)
                    cx.op("dve", lambda e, s=s, cs=cs, g=g: e.scalar_tensor_tensor(
                        out=pT[:, cs, :], in0=s[:, cs, 16:XW], scalar=1.0 / WINDOWS[g], in1=u[:, cs, 16:XW], op0=ALU.mult, op1=ALU.subtract),
                        reads=[s, u], writes=[pT])
                    if bk == 0 and sh == cfg.SH0:
                        cx.op("dve", lambda e, s=s, cs=cs, g=g: e.tensor_tensor(
                            out=s[:, cs, 16:32], in0=s[:, cs, 16:32], in1=corr[:, g:g + 1, :].to_broadcast([128, cpg, 16]), op=ALU.mult),
                            reads=[s, corr], writes=[s])
                        cx.op("dve", lambda e, s=s, cs=cs: e.tensor_tensor(out=pT[:, cs, 0:16], in0=s[:, cs, 16:32], in1=u[:, cs, 16:32], op=ALU.subtract),
                              reads=[s, u], writes=[pT])
                for g in range(4):
                    for oc in range(GDC):
                        pb = PB[2 + (g * GDC + oc) % 2]
                        for ic in range(GDC):
                            mm(pb, pb[:, 0:TB], pw, pw[:, g * GDC + ic, oc * 128:(oc + 1) * 128], pT, pT[:, g * GDC + ic, :], ic == 0, ic == GDC - 1)
                        ch = g * GDC + oc
                        cx.op("act", lambda e, ch=ch, pb=pb: e.activation(out=ypT[:, ch, :], in_=pb[:, 0:TB], func=AF.Copy, scale=psc[:, ch:ch + 1]),
                              reads=[pb, psc], writes=[ypT])

            for br in range(2):
                for j in range(D // 256):
                    def load(j=j, br=br):
                        wb_, nk = (wbp16, PWC) if br == 0 else (wba16, H)
                        st["wb", br, j] = rslot(nk, 256)
                        b, v = st["wb", br, j]
                        wload16(b, v, wb_, kview(wb_, 0, nk, j * 256, 256))
                        st["wg", br, j] = rslot(KC, 256)
                        b, v = st["wg", br, j]
                        wload16(b, v, Wi16, kview(Wi16, 0, KC, PW + br * D + j * 256, 256))

                    def comp(j=j, br=br):
                        if br == 0 and j == 0:
                            pool_mix()
                        src, nk = (ypT, PWC) if br == 0 else (atT, H)
                        b, v = st["wb", br, j]; bg, vg = st["wg", br, j]
                        for oc in range(2):
                            ch = 2 * j + oc
                            pa = PB[oc]; pg = PB[2 + oc]; s = sg[oc]
                            for c in range(nk):
                                mm(pa, pa[:, 0:TB], b, v[:, c, oc * 128:(oc + 1) * 128], src, src[:, c, :], c == 0, c == nk - 1)
                            for c in range(KC):
                                mm(pg, pg[:, 0:TB], bg, vg[:, c, oc * 128:(oc + 1) * 128], xTe, xTe[:, c, 16:XW], c == 0, c == KC - 1)
                            cx.op("act", lambda e, s=s, pg=pg: e.activation(out=s[:], in_=pg[:, 0:TB], func=AF.Sigmoid), reads=[pg], writes=[s])
                            if br == 0:
                                cx.op("dve", lambda e, s=s, pa=pa, ch=ch: e.tensor_tensor(out=mg[:, ch, :], in0=pa[:, 0:TB], in1=s[:], op=ALU.mult), reads=[pa, s], writes=[mg])
                            else:
                                cx.op("dve", lambda e, s=s, pa=pa: e.tensor_tensor(out=tmpf[:], in0=pa[:, 0:TB], in1=s[:], op=ALU.mult), reads=[pa, s], writes=[tmpf])
                                cx.op("dve", lambda e, ch=ch: e.tensor_tensor(out=mg[:, ch, :], in0=mg[:, ch, :], in1=tmpf[:], op=ALU.add), reads=[mg, tmpf], writes=[mg])
                    add_unit(load, comp)

            def ln_router():
                for t in range(TT):
                    tok0 = o0 + t * 128
                    tile_i = tok0 // 128
                    ln1(xin4, xin4[:, t, :])
                    cx.op("act", lambda e, t=t: e.copy(out=hb[:], in_=xin4[:, t, :]), reads=[xin4], writes=[hb])
                    dma(sp, H16, H16.t[tok0:tok0 + 128, :], hb, hb[:], sem=hb)
                    transpose_rows(hb, hb[:], 128, KC, hT, lambda c0, n, t=t: hT[:, c0:c0 + n, t * 128:(t + 1) * 128])
                    for g0 in range(0, KC, 4):
                        pb = PB[4 + (g0 // 4) % 2]
                        for i in range(4):
                            c = g0 + i
                            cx.op("pe", lambda e, pb=pb, i=i, c=c, t=t: e.transpose(out=pb[:, i * 128:(i + 1) * 128], in_=xin4[:, t, c * 128:(c + 1) * 128], identity=ident32),
                                  reads=[xin4, cstf], writes=[pb])
                        evac(hT32, hT32[:, g0:g0 + 4, :], pb, pb[:, :].rearrange("p (c n) -> p c n", n=128))
                    pr = PB[0]
                    for c in range(KC):
                        mm(pr, pr[:, 0:E], hT32, hT32[:, c, :], wr, wr[:, c, :], c == 0, c == KC - 1)
                    sc, selb, mask, pos, oh, tmp = (rt[k] for k in ("sc", "selb", "mask", "pos", "oh", "tmp"))
                    cx.op("act", lambda e: e.activation(out=sc[:], in_=pr[:, 0:E], func=AF.Sigmoid), reads=[pr], writes=[sc])
                    cx.op("dve", lambda e: e.tensor_tensor(out=selb[:], in0=sc[:], in1=rbias[:], op=ALU.add), reads=[sc, rbias], writes=[selb])
                    cx.op("dve", lambda e: e.max(out=m8[:], in_=selb[:]), reads=[selb], writes=[m8])
                    cx.op("dve", lambda e: e.tensor_scalar(out=mask[:], in0=selb[:], scalar1=m8[:, TOPK - 1:TOPK], scalar2=None, op0=ALU.is_ge), reads=[selb, m8], writes=[mask])
                    pp = PB[1]
                    mm(pp, pp[:, 0:E], cstf, Us32, mask, mask[:], True, True)
                    mm(pp, pp[:, 128:128 + E], cstf, ones32, mask, mask[:], True, True)
                    cx.op("dve", lambda e: e.tensor_tensor(out=pos[:], in0=pp[:, 0:E], in1=rcar[:], op=ALU.add), reads=[pp, rcar], writes=[pos])
                    cx.op("dve", lambda e: e.tensor_tensor(out=rcar[:], in0=pp[:, 128:128 + E], in1=rcar[:], op=ALU.add), reads=[pp, rcar], writes=[rcar])
                    for k in range(TOPK):
                        cx.op("dve", lambda e, k=k: e.tensor_scalar(out=oh[:], in0=selb[:], scalar1=m8[:, k:k + 1], scalar2=None, op0=ALU.is_equal), reads=[selb, m8], writes=[oh])
                        for qi, srcb in enumerate((sc, pos, None)):
                            sap = iotaE if srcb is None else srcb[:]
                            cx.op("dve", lambda e, sap=sap: e.tensor_tensor(out=tmp[:], in0=oh[:], in1=sap, op=ALU.mult), reads=[oh, cstf] + ([srcb] if srcb else []), writes=[tmp])
                            cx.op("dve", lambda e, qi=qi, k=k: e.reduce_sum(out=r3[:, qi, k:k + 1], in_=tmp[:], axis=AX.X), reads=[tmp], writes=[r3])
                    cx.op("dve", lambda e: e.reduce_sum(out=rsum[:, 0:1], in_=r3[:, 0, 0:TOPK], axis=AX.X), reads=[r3], writes=[rsum])
                    cx.op("dve", lambda e: e.reciprocal(out=rsum[:, 1:2], in_=rsum[:, 0:1]), reads=[rsum], writes=[rsum])
                    cx.op("dve", lambda e, tile_i=tile_i: e.tensor_scalar(out=gates[:, tile_i, 0:TOPK], in0=r3[:, 0, 0:TOPK], scalar1=rsum[:, 1:2], scalar2=None, op0=ALU.mult),
                          reads=[r3, rsum], writes=[gates])
                    cx.op("dve", lambda e: e.tensor_scalar(out=r3[:, 1, 0:TOPK], in0=r3[:, 1, 0:TOPK], scalar1=float(C - 1), scalar2=None, op0=ALU.min), reads=[r3], writes=[r3])
                    cx.op("dve", lambda e: e.scalar_tensor_tensor(out=slf[:, 0:TOPK], in0=r3[:, 2, 0:TOPK], scalar=float(C), in1=r3[:, 1, 0:TOPK], op0=ALU.mult, op1=ALU.add),
                          reads=[r3], writes=[slf])
                    cx.op("dve", lambda e, tile_i=tile_i: e.tensor_copy(out=slots[:, tile_i, 0:TOPK], in_=slf[:, 0:TOPK]), reads=[slf], writes=[slots])

            for j in range(D // 256):
                def load(j=j):
                    st["wo", j] = rslot(KC, 256)
                    b, v = st["wo", j]
                    wload16(b, v, wo16, kview(wo16, 0, KC, j * 256, 256))

                def comp(j=j):
                    b, v = st["wo", j]
                    for t in range(TT):
                        pb = PB[t % 2]
                        for c in range(KC):
                            mm(pb, pb[:, 0:256], mg, mg[:, c, t * 128:(t + 1) * 128], b, v[:, c, :], c == 0, c == KC - 1)
                        cx.op("dve", lambda e, t=t, pb=pb, j=j: e.scalar_tensor_tensor(
                            out=xin4[:, t, j * 256:(j + 1) * 256], in0=xin4[:, t, j * 256:(j + 1) * 256], scalar=ALPHA, in1=pb[:, 0:256], op0=ALU.mult, op1=ALU.add),
                            reads=[xin4, pb], writes=[xin4])
                    if j == D // 256 - 1:
                        ln_router()
                add_unit(load, comp)

            for j in range(FS // 256):
                def load(j=j):
                    st["s1", j] = rslot(KC, 256); b, v = st["s1", j]
                    wload16(b, v, ws1b, kview(ws1b, 0, KC, j * 256, 256))
                    st["s3", j] = rslot(KC, 256); b, v = st["s3", j]
                    wload16(b, v, ws3b, kview(ws3b, 0, KC, j * 256, 256))

                def comp(j=j):
                    b1, v1 = st["s1", j]; b3, v3 = st["s3", j]
                    for oc in range(2):
                        p1 = PB[oc]; p3 = PB[2 + oc]; s = sg[oc]
                        for c in range(KC):
                            mm(p1, p1[:, 0:TB], b1, v1[:, c, oc * 128:(oc + 1) * 128], hT, hT[:, c, :], c == 0, c == KC - 1)
                        for c in range(KC):
                            mm(p3, p3[:, 0:TB], b3, v3[:, c, oc * 128:(oc + 1) * 128], hT, hT[:, c, :], c == 0, c == KC - 1)
                        cx.op("act", lambda e, s=s, p1=p1: e.activation(out=s[:], in_=p1[:, 0:TB], func=AF.Silu), reads=[p1], writes=[s])
                        cx.op("dve", lambda e, s=s, p3=p3, ch=2 * j + oc: e.tensor_tensor(out=aT[:, ch, :], in0=p3[:, 0:TB], in1=s[:], op=ALU.mult), reads=[p3, s], writes=[aT])
                add_unit(load, comp)

            for j in range(D // 256):
                def load(j=j):
                    st["s2", j] = rslot(FSC, 256); b, v = st["s2", j]
                    wload16(b, v, ws2b, kview(ws2b, 0, FSC, j * 256, 256))

                def comp(j=j):
                    b, v = st["s2", j]
                    for t in range(TT):
                        pb = PB[4 + t % 2]
                        for c in range(FSC):
                            mm(pb, pb[:, 0:256], aT, aT[:, c, t * 128:(t + 1) * 128], b, v[:, c, :], c == 0, c == FSC - 1)
                        cx.op("dve", lambda e, t=t, pb=pb, j=j: e.scalar_tensor_tensor(
                            out=xin4[:, t, j * 256:(j + 1) * 256], in0=xin4[:, t, j * 256:(j + 1) * 256], scalar=ALPHA, in1=pb[:, 0:256], op0=ALU.mult, op1=ALU.add),
                            reads=[xin4, pb], writes=[xin4])
                    if j == D // 256 - 1:
                        for t in range(TT):
                            tok0 = o0 + t * 128
                            dma(sp, Base, Base.t[tok0:tok0 + 128, :], xin4, xin4[:, t, :], sem=xin4)
                            for k in range(TOPK):
                                ti_ = tok0 // 128
                                cx.dma(pool, lambda e, ti_=ti_, k=k: e.indirect_dma_start(
                                    out=Tab.t[:, :], out_offset=bass.IndirectOffsetOnAxis(ap=slots[:, ti_, k:k + 1], axis=0),
                                    in_=tokid[:, ti_:ti_ + 1], in_offset=None), reads=[slots, tokid, Tab], writes=[], sem=slots)
                add_unit(load, comp)

        def run_units(depth):
            n = len(units)
            for i in range(n + depth):
                if i < n:
                    units[i][0]()
                if i >= depth:
                    units[i - depth][1]()

        for bk in range(NB):
            block_units(bk)
        run_units(1)

        cx.phase()
        units.clear()
        CT = C // 128
        RS5 = max(KC, FC) * 256
        ring5 = [cx.alloc("r5_%d" % i, [128, RS5], BF16) for i in range(6)]
        r5 = {"n": 0}

        def rslot5(nch, ncols):
            b = ring5[r5["n"] % len(ring5)]; r5["n"] += 1
            return b, b[:, 0:nch * ncols].rearrange("p (c n) -> p c n", n=ncols)

        tokt = [cx.alloc("tokt%d" % i, [128, CT], I32) for i in range(2)]
        xg = [cx.alloc("xg%d" % i, [128, D], BF16) for i in range(4)]
        XeT = [cx.alloc("XeT%d" % i, [128, KC, C], BF16) for i in range(2)]
        aTe = cx.alloc("aTe", [128, FC, C], BF16)
        st5 = [cx.alloc("st5_%d" % i, [128, C], F32) for i in range(2)]
        yst = [cx.alloc("yst%d" % i, [128, CT, 256], F32) for i in range(3)]
        gcnt = {"n": 0, "y": 0}

        def expert_units(ex):
            st = {}
            xe = XeT[ex % 2]

            def gather():
                tk = tokt[ex % 2]
                for i in range(CT):
                    dma(sp, tk, tk[:, i:i + 1], Tab, Tab.t[ex * C + i * 128:ex * C + (i + 1) * 128, :])
                for i in range(CT):
                    g = xg[gcnt["n"] % 4]; gcnt["n"] += 1
                    cx.dma(pool, lambda e, g=g, tk=tk, i=i: e.indirect_dma_start(
                        out=g[:], out_offset=None, in_=H16.t[:, :], in_offset=bass.IndirectOffsetOnAxis(ap=tk[:, i:i + 1], axis=0)),
                        reads=[tk, H16], writes=[g])
                    transpose_rows(g, g[:], 128, KC, xe, lambda c0, n, i=i: xe[:, c0:c0 + n, i * 128:(i + 1) * 128])

            n13 = (F + 255) // 256
            for j in range(n13):
                ncols = min(256, F - j * 256)

                def load(j=j, ncols=ncols):
                    if j == 0:
                        gather()
                    st["w1", j] = rslot5(KC, ncols); b, v = st["w1", j]
                    wload(b, v, w1, w1.t[ex, :, j * 256:j * 256 + ncols].rearrange("(c p) n -> p c n", p=128))
                    st["w3", j] = rslot5(KC, ncols); b, v = st["w3", j]
                    wload(b, v, w3, w3.t[ex, :, j * 256:j * 256 + ncols].rearrange("(c p) n -> p c n", p=128))

                def comp(j=j, ncols=ncols):
                    b1, v1 = st["w1", j]; b3, v3 = st["w3", j]
                    for oc in range(ncols // 128):
                        p1 = PB[oc]; p3 = PB[2 + oc]; s = st5[oc]
                        for c in range(KC):
                            mm(p1, p1[:, 0:C], b1, v1[:, c, oc * 128:(oc + 1) * 128], xe, xe[:, c, :], c == 0, c == KC - 1)
                        for c in range(KC):
                            mm(p3, p3[:, 0:C], b3, v3[:, c, oc * 128:(oc + 1) * 128], xe, xe[:, c, :], c == 0, c == KC - 1)
                        cx.op("act", lambda e, s=s, p1=p1: e.activation(out=s[:], in_=p1[:, 0:C], func=AF.Silu), reads=[p1], writes=[s])
                        cx.op("dve", lambda e, s=s, p3=p3, ch=2 * j + oc: e.tensor_tensor(out=aTe[:, ch, :], in0=p3[:, 0:C], in1=s[:], op=ALU.mult), reads=[p3, s], writes=[aTe])
                add_unit(load, comp)

            for j in range(D // 256):
                def load(j=j):
                    st["w2", j] = rslot5(FC, 256); b, v = st["w2", j]
                    wload(b, v, w2, w2.t[ex, :, j * 256:(j + 1) * 256].rearrange("(c p) n -> p c n", p=128))

                def comp(j=j):
                    b, v = st["w2", j]
                    ys = yst[gcnt["y"] % 3]; gcnt["y"] += 1
                    for i in range(CT):
                        pb = PB[4 + i % 2]
                        for c in range(FC):
                            mm(pb, pb[:, 0:256], aTe, aTe[:, c, i * 128:(i + 1) * 128], b, v[:, c, :], c == 0, c == FC - 1)
                        evac(ys, ys[:, i, :], pb, pb[:, 0:256])
                    dma(sp, Yb, Yb.t[ex * C:(ex + 1) * C, j * 256:(j + 1) * 256].rearrange("(i p) d -> p i d", p=128), ys, ys[:], sem=ys)
                add_unit(load, comp)

        for ex in range(E):
            expert_units(ex)
        run_units(2)

        cx.phase()
        ln2g = cx.alloc("ln2g", [128, D], F32); ln2b = cx.alloc("ln2b", [128, D], F32)
        dma(sp, ln2g, ln2g[:], ln2g_d, ln2g_d[:, :]); dma(sp, ln2b, ln2b[:], ln2b_d, ln2b_d[:, :])
        ln2 = layernorm(None, None, ln2g, ln2b, "2")
        acc = [cx.alloc("acc%d" % i, [128, D], F32) for i in range(2)]
        yk = [cx.alloc("yk%d" % i, [128, D], F32) for i in range(8)]
        ykc = 0
        for t in range(NT):
            a = acc[t % 2]
            dma(sp, a, a[:], Base, Base.t[t * 128:(t + 1) * 128, :])
            for k in range(TOPK):
                y = yk[ykc % 8]; ykc += 1
                cx.dma(pool, lambda e, y=y, t=t, k=k: e.indirect_dma_start(
                    out=y[:], out_offset=None, in_=Yb.t[:, :], in_offset=bass.IndirectOffsetOnAxis(ap=slots[:, t, k:k + 1], axis=0)),
                    reads=[slots, Yb], writes=[y])
                cx.op("dve", lambda e, y=y, a=a, t=t, k=k: e.scalar_tensor_tensor(
                    out=a[:], in0=y[:], scalar=gates[:, t, k:k + 1], in1=a[:], op0=ALU.mult, op1=ALU.add), reads=[y, a, gates], writes=[a])
            ln2(a, a[:])
            dma(sp, out_d, out_d.t[(sh - cfg.SH0) * TO + t * 128:(sh - cfg.SH0) * TO + (t + 1) * 128, :], a, a[:], sem=a)

    for sh in range(cfg.SH0, cfg.NSH):
        shard_phases(sh)

    cx.emit()
    return nc


def make_consts(cfg):
    E = cfg.E
    k = np.arange(128)[:, None]; m = np.arange(128)[None, :]
    cst = np.zeros((128, 5 * 128 + E), np.float32)
    cst[:, 0:128] = np.eye(128)
    cst[:, 128:256] = (k <= m)
    cst[:, 256:384] = (k < m)
    cst[:, 384:512] = 1.0
    cst[0, 512:640] = 1.0
    cst[:, 640:640 + E] = np.arange(E)[None, :]
    tokid = (np.arange(cfg.NT)[None, :] * 128 + np.arange(128)[:, None]).astype(np.int32)
    return cst, tokid


def prep_inputs(cfg, inp):
    D, TO, NV, H, E = cfg.D, cfg.TO, cfg.NV, cfg.H, cfg.E
    x = np.asarray(inp["x"], np.float32)
    cst, tokid = make_consts(cfg)

    def rep(v, n):
        return np.ascontiguousarray(np.broadcast_to(np.asarray(v, np.float32).reshape(1, n), (128, n)))
    shared = {
        "cst": cst, "tokid": tokid,
        "w_in": np.ascontiguousarray(inp["w_in"][0]), "bfor": rep(inp["b_forget"][0], H),
        "pool_w": np.ascontiguousarray(inp["pool_w"][0]),
        "pscale": np.ascontiguousarray(np.asarray(inp["pool_scale"][0], np.float32).reshape(cfg.PWC, 128).T),
        "wbp": np.ascontiguousarray(inp["w_branch_pool"][0]), "wba": np.ascontiguousarray(inp["w_branch_attn"][0]),
        "w_out": np.ascontiguousarray(inp["w_out"][0]), "ln1g": rep(inp["ln1_g"][0], D), "ln1b": rep(inp["ln1_b"][0], D),
        "w_router": np.ascontiguousarray(inp["w_router"][0]), "rbias": rep(inp["router_bias"][0], E),
        "w1": np.ascontiguousarray(inp["w1"][0]), "w3": np.ascontiguousarray(inp["w3"][0]), "w2": np.ascontiguousarray(inp["w2"][0]),
        "ws1": np.ascontiguousarray(inp["w_shared1"][0]), "ws3": np.ascontiguousarray(inp["w_shared3"][0]),
        "ws2": np.ascontiguousarray(inp["w_shared2"][0]), "ln2g": rep(inp["ln2_g"][0], D), "ln2b": rep(inp["ln2_b"][0], D),
    }
    maps = []
    NOWN = (cfg.NSH - cfg.SH0) * TO
    per_b = cfg.NCORES // 2
    for c in range(cfg.NCORES):
        b, j = c // per_b, c % per_b
        end = (j + 1) * NOWN
        xvv = np.zeros((NV, D), np.float32)
        kb = np.full((NV,), NEG, np.float32)
        n_real = min(end, NV)
        xvv[NV - n_real:] = x[b, end - n_real:end]
        kb[NV - n_real:] = 0.0
        kbias = np.ascontiguousarray(kb.reshape(cfg.NKT, 128).T)
        corr = np.zeros((128, 64), np.float32)
        for g, w in enumerate(WINDOWS):
            pos = j * NOWN + np.arange(16)
            corr[:, g * 16:(g + 1) * 16] = (1.0 / np.minimum(pos + 1, w))[None, :]
        m = dict(shared)
        m.update({"xv": xvv, "kbias": kbias, "corr": corr})
        maps.append(m)
    return maps


_NC_CACHE = {}


def run(cfg, inp):
    key = (cfg.D, cfg.TO, cfg.E, cfg.TOPK, cfg.F, cfg.FS, cfg.C, cfg.TB, cfg.NSH, cfg.SH0)
    if key not in _NC_CACHE:
        _NC_CACHE[key] = build(cfg)
    nc = _NC_CACHE[key]
    maps = prep_inputs(cfg, inp)
    res = run_bass_kernel_spmd(nc, maps, core_ids=list(range(len(maps))))
    outs = [np.asarray(r["out"], np.float32) for r in res.results]
    per_b = cfg.NCORES // 2
    return np.stack([np.concatenate(outs[b * per_b:(b + 1) * per_b], axis=0) for b in range(2)], axis=0)


def kernel(**inputs):
    return run(Cfg(), inputs)
```
